# Optimizing a Trainium2 kernel written in Bass

```python
import math
import jax, jax.numpy as jnp
from jax import lax
import numpy as np

D_MODEL = 1024
BATCH = 32
SEQ = 2048
DEPTH = 4

CHUNK = 64
PLE_DIM = 256
MIX_WIDTH = D_MODEL
POOL_WIDTH = MIX_WIDTH // 2
SSM_WIDTH = MIX_WIDTH - POOL_WIDTH
POOL_WINDOWS = (2, 4, 8, 16)
POOL_GROUP = POOL_WIDTH // len(POOL_WINDOWS)
SSM_GROUP_CH = 16
SSM_GROUPS = SSM_WIDTH // SSM_GROUP_CH
SSM_STATE = 64
DT_MIN = 1e-3
DT_MAX = 1e-1
A_RE_MAX = -1e-4
_FF_RAW = -(-(8 * D_MODEL) // 3)
D_FF = -(-_FF_RAW // 256) * 256
DEEPNORM_ALPHA = (2.0 * DEPTH) ** 0.25
DEEPNORM_BETA = (8.0 * DEPTH) ** -0.25
LN_EPS = 1e-5

kernel_name = "hybrid_pool_s5_deepnorm_encoder"


def layer_norm(x, g, b):
    xf = x.astype(jnp.float32)
    mu = jnp.mean(xf, axis=-1, keepdims=True)
    var = jnp.mean(jnp.square(xf - mu), axis=-1, keepdims=True)
    y = (xf - mu) * lax.rsqrt(var + LN_EPS) * g.astype(jnp.float32) + b.astype(jnp.float32)
    return y.astype(x.dtype)


def multiscale_pool(u, w, b, scale):
    bsz, s, _ = u.shape
    uf = u.astype(jnp.float32)
    pos = jnp.arange(1, s + 1, dtype=jnp.float32)[None, :, None]
    outs = []
    for g, win in enumerate(POOL_WINDOWS):
        ug = uf[..., g * POOL_GROUP:(g + 1) * POOL_GROUP]
        c = jnp.cumsum(ug, axis=1)
        lag = jnp.pad(c[:, :-win], ((0, 0), (win, 0), (0, 0)))
        mean = (c - lag) / jnp.minimum(pos, float(win))
        outs.append(mean - ug)
    z = jnp.stack(outs, axis=2)
    y = jnp.einsum('bsgc,gcd->bsgd', z, w.astype(jnp.float32)).reshape(bsz, s, POOL_WIDTH)
    return (y + b.astype(jnp.float32)) * scale.astype(jnp.float32)


def s5_mixer(u, a_re, a_im, log_dt, b_re, b_im, c_re, c_im, d, glu_w, glu_b):
    bsz, s, _ = u.shape
    f32 = jnp.float32
    uf = u.astype(f32)
    ug = uf.reshape(bsz, s, SSM_GROUPS, SSM_GROUP_CH)
    lam = lax.complex(jnp.minimum(a_re.astype(f32), A_RE_MAX), a_im.astype(f32))
    dt = jnp.exp(log_dt.astype(f32))[:, None]
    lam_bar = jnp.exp(lam * dt)
    b_mat = lax.complex(b_re.astype(f32), b_im.astype(f32))
    b_bar = ((lam_bar - 1.0) / lam)[..., None] * b_mat
    bu = lax.complex(jnp.einsum('bsgc,gnc->bsgn', ug, jnp.real(b_bar)),
                     jnp.einsum('bsgc,gnc->bsgn', ug, jnp.imag(b_bar)))
    a_seq = jnp.broadcast_to(lam_bar, (1, s) + lam_bar.shape)

    def combine(left, right):
        a_l, x_l = left
        a_r, x_r = right
        return a_r * a_l, a_r * x_l + x_r

    _, states = lax.associative_scan(combine, (a_seq, bu), axis=1)
    y = (jnp.einsum('bsgn,gcn->bsgc', jnp.real(states), c_re.astype(f32))
         - jnp.einsum('bsgn,gcn->bsgc', jnp.imag(states), c_im.astype(f32)))
    y = y.reshape(bsz, s, SSM_WIDTH) + d.astype(f32) * uf
    y = jax.nn.gelu(y)
    return y * jax.nn.sigmoid(y @ glu_w.astype(f32) + glu_b.astype(f32))


def setup_inputs(seed: int = 0) -> dict:
    key = jax.random.key(seed)
    ks = jax.random.split(key, 27)
    L, D, M, H = DEPTH, D_MODEL, MIX_WIDTH, D_FF
    G, N, C = SSM_GROUPS, SSM_STATE, SSM_GROUP_CH
    nrm = lambda k, shape, std: jax.random.normal(k, shape, jnp.float32) * std
    xavier = lambda fi, fo: math.sqrt(2.0 / (fi + fo))
    n_idx = jnp.arange(N, dtype=jnp.float32)
    return {
        "x": nrm(ks[0], (BATCH, SEQ, D), 1.0),
        "p": nrm(ks[1], (L, BATCH, SEQ, PLE_DIM), 1.0),
        "w_in": nrm(ks[2], (L, D, M), D ** -0.5),
        "pool_w": nrm(ks[3], (L, len(POOL_WINDOWS), POOL_GROUP, POOL_GROUP), POOL_GROUP ** -0.5),
        "pool_b": nrm(ks[4], (L, POOL_WIDTH), 0.01),
        "pool_scale": 1.0 + nrm(ks[5], (L, POOL_WIDTH), 0.02),
        "ssm_a_re": -0.5 + nrm(ks[6], (L, G, N), 0.01),
        "ssm_a_im": math.pi * n_idx[None, None, :] + nrm(ks[7], (L, G, N), 0.01),
        "ssm_log_dt": jax.random.uniform(ks[8], (L, G), jnp.float32, math.log(DT_MIN), math.log(DT_MAX)),
        "ssm_b_re": nrm(ks[9], (L, G, N, C), (2.0 * C) ** -0.5),
        "ssm_b_im": nrm(ks[10], (L, G, N, C), (2.0 * C) ** -0.5),
        "ssm_c_re": nrm(ks[11], (L, G, C, N), N ** -0.5),
        "ssm_c_im": nrm(ks[12], (L, G, C, N), N ** -0.5),
        "ssm_d": nrm(ks[13], (L, SSM_WIDTH), 1.0),
        "ssm_glu_w": nrm(ks[14], (L, SSM_WIDTH, SSM_WIDTH), SSM_WIDTH ** -0.5),
        "ssm_glu_b": nrm(ks[15], (L, SSM_WIDTH), 0.01),
        "w_out": nrm(ks[16], (L, M, D), xavier(M, D) * DEEPNORM_BETA),
        "ln1_g": 1.0 + nrm(ks[17], (L, D), 0.02),
        "ln1_b": nrm(ks[18], (L, D), 0.01),
        "ffn_w1": nrm(ks[19], (L, D, H), xavier(D, H)),
        "ffn_w3": nrm(ks[20], (L, D, H), xavier(D, H)),
        "ffn_w2": nrm(ks[21], (L, H, D), xavier(H, D) * DEEPNORM_BETA),
        "ple_w": nrm(ks[22], (L, PLE_DIM, D), xavier(PLE_DIM, D) * DEEPNORM_BETA),
        "ple_gate_w": nrm(ks[23], (L, D, D), D ** -0.5),
        "ln2_g": 1.0 + nrm(ks[24], (L, D), 0.02),
        "ln2_b": nrm(ks[25], (L, D), 0.01),
    }


def reference(x, p, w_in, pool_w, pool_b, pool_scale, ssm_a_re, ssm_a_im, ssm_log_dt,
              ssm_b_re, ssm_b_im, ssm_c_re, ssm_c_im, ssm_d, ssm_glu_w, ssm_glu_b,
              w_out, ln1_g, ln1_b, ffn_w1, ffn_w3, ffn_w2, ple_w, ple_gate_w, ln2_g, ln2_b):
    h = x
    for i in range(DEPTH):
        u = h @ w_in[i]
        y_pool = multiscale_pool(u[..., :POOL_WIDTH], pool_w[i], pool_b[i], pool_scale[i])
        y_ssm = s5_mixer(u[..., POOL_WIDTH:], ssm_a_re[i], ssm_a_im[i], ssm_log_dt[i],
                         ssm_b_re[i], ssm_b_im[i], ssm_c_re[i], ssm_c_im[i], ssm_d[i],
                         ssm_glu_w[i], ssm_glu_b[i])
        mix = jnp.concatenate([y_pool, y_ssm], axis=-1).astype(h.dtype) @ w_out[i]
        h = layer_norm(DEEPNORM_ALPHA * h + mix, ln1_g[i], ln1_b[i])
        f = (jax.nn.silu(h @ ffn_w1[i]) * (h @ ffn_w3[i])) @ ffn_w2[i]
        r = DEEPNORM_ALPHA * h + f
        e = (p[i] @ ple_w[i]) * jax.nn.sigmoid(r @ ple_gate_w[i])
        h = layer_norm(r + e, ln2_g[i], ln2_b[i])
    return h
```

```python
import contextlib
import math
import os
PH = int(os.environ.get('KPH', '99'))
KSKIP = os.environ.get('KSKIP', '')
import numpy as np
import concourse.bass as bass
import concourse.mybir as mybir
from concourse.bass_utils import run_bass_kernel_spmd

F32 = mybir.dt.float32
BF16 = mybir.dt.bfloat16
I32 = mybir.dt.int32
AF = mybir.ActivationFunctionType
ALU = mybir.AluOpType
ESZ = {F32: 4, BF16: 2, I32: 4}
ENGS = ("pe", "act", "dve", "pool", "sp")
PAGE = 128

NCORES = 8
DEPTH = 4
SEQ = 2048
D = 1024
DFF = 2816
NT = 4
TW = 512
ALPHA = (2.0 * DEPTH) ** 0.25
LN_EPS = 1e-5
WINS = (2, 4, 8, 16)
STAGES = [(1, 4), (4, 4), (16, 4), (64, 4), (256, 4), (1024, 2)]
PWLIST = []
for _s, _r in STAGES:
    for _k in range(1, _r):
        PWLIST.append(_s * _k)
PWIDX = {p: i for i, p in enumerate(PWLIST)}
FPASS = [(0, 8), (8, 8), (16, 6)]


class V:
    __slots__ = ("ap", "space", "lo", "hi")

    def __init__(self, ap, space, lo, hi):
        self.ap, self.space, self.lo, self.hi = ap, space, lo, hi


class Buf:
    def __init__(self, full_ap, space, base, cols, dtype):
        self.full, self.space, self.base, self.cols, self.dtype = full_ap, space, base, cols, dtype
        self.esz = ESZ[dtype]

    def v(self, c0=0, c1=None):
        c1 = self.cols if c1 is None else c1
        assert 0 <= c0 < c1 <= self.cols, (c0, c1, self.cols)
        return V(self.full[:, c0:c1], self.space, self.base + c0 * self.esz, self.base + c1 * self.esz)

    def v3(self, c0, n, stride, inner, i0=0):
        c0 += i0
        a0, r = divmod(c0, stride)
        assert self.cols % stride == 0 and r + inner <= stride and a0 + n <= self.cols // stride, (c0, n, stride, inner, self.cols)
        ap = self.full.rearrange("p (a b) -> p a b", b=stride)[:, a0:a0 + n, r:r + inner]
        c0 -= i0
        lo = self.base + (c0 + i0) * self.esz
        hi = self.base + (c0 + (n - 1) * stride + i0 + inner) * self.esz
        return V(ap, self.space, lo, hi)

    def parts(self, c0, n, stride, inner):
        return [self.v(c0 + a * stride, c0 + a * stride + inner) for a in range(n)]


class Rec:
    __slots__ = ("lo", "hi", "w", "sig", "dead")

    def __init__(self, lo, hi, w, sig):
        self.lo, self.hi, self.w, self.sig, self.dead = lo, hi, w, sig, False


class Op:
    __slots__ = ("fn", "waits", "sig", "inc")

    def __init__(self, fn, waits, sig, inc):
        self.fn, self.waits, self.sig, self.inc = fn, waits, sig, inc


class KB:
    def __init__(self, nc, sb_bytes, stack):
        self.nc = nc
        self.stack = stack
        self.ops = {e: [] for e in ENGS}
        self.pages = {"sb": {}, "ps": {}}
        self.waited = {e: {} for e in ENGS}
        self.sem_of = {}
        self.cnt = {}
        self.nsem = 0
        self.sb_words = sb_bytes // 4
        self.arena = stack.enter_context(nc.sbuf_tensor("arena", [128, self.sb_words], F32))
        self.psum = stack.enter_context(nc.psum_tensor("psum", [128, 8 * 512], F32))
        self.sb_top = 0
        self.nbank = 0
        self.new_epoch()

    def alloc(self, cols, dtype):
        nbytes = (cols * ESZ[dtype] + 127) // 128 * 128
        base = self.sb_top
        self.sb_top += nbytes
        assert self.sb_top <= self.sb_words * 4, ("SBUF overflow", self.sb_top)
        return self.at(base, cols, dtype)

    def at(self, base, cols, dtype):
        assert base % 4 == 0
        w0 = base // 4
        nw = (cols * ESZ[dtype] + 3) // 4
        assert (w0 + nw) <= self.sb_words, ("SBUF overflow at", base, cols)
        ap = self.arena[:, w0:w0 + nw]
        if dtype != F32:
            ap = ap.bitcast(dtype)
        return Buf(ap, "sb", base, cols, dtype)

    def bank(self, i, dtype=F32):
        ap = self.psum[:, i * 512:(i + 1) * 512]
        cols = 512
        if dtype != F32:
            ap = ap.bitcast(dtype)
            cols = 512 * 4 // ESZ[dtype]
        return Buf(ap, "ps", i * 2048, cols, dtype)

    def nb(self, dtype=F32):
        b = self.bank(self.nbank % 8, dtype)
        self.nbank += 1
        return b

    def _newsem(self, name):
        s = self.stack.enter_context(self.nc.semaphore(f"{name}_{self.nsem}"))
        self.nsem += 1
        self.cnt[id(s)] = 0
        return s

    def new_epoch(self):
        for e in ("pe", "act", "dve", "pool"):
            self.sem_of[e] = self._newsem(e)

    def chan(self, name="dma"):
        return self._newsem(name)

    def _cands(self, v):
        pg = self.pages[v.space]
        seen = {}
        for p in range(v.lo // PAGE, (v.hi - 1) // PAGE + 1):
            lst = pg.get(p)
            if lst:
                for r in lst:
                    if not r.dead and r.lo < v.hi and v.lo < r.hi:
                        seen[id(r)] = r
        return seen.values()

    def _add(self, v, w, sig):
        r = Rec(v.lo, v.hi, w, sig)
        pg = self.pages[v.space]
        for p in range(v.lo // PAGE, (v.hi - 1) // PAGE + 1):
            lst = pg.get(p)
            if lst is None:
                pg[p] = [r]
            else:
                if len(lst) > 8:
                    lst[:] = [x for x in lst if not x.dead]
                lst.append(r)

    def emit(self, eng, fn, reads=(), writes=(), chan=None):
        if any(v.space == "ps" for v in reads) or any(v.space == "ps" for v in writes):
            nr, nw, seenb = [], [], set()
            for v in list(reads) + list(writes):
                if v.space == "ps":
                    for b in range(v.lo // 2048, (v.hi - 1) // 2048 + 1):
                        if b not in seenb:
                            seenb.add(b)
                            nw.append(V(None, "ps", b * 2048, b * 2048 + 2048))
            nr = [v for v in reads if v.space != "ps"]
            nw = nw + [v for v in writes if v.space != "ps"]
            reads, writes = nr, nw
        deps = {}
        for v in reads:
            for r in self._cands(v):
                if r.w:
                    k = id(r.sig[0])
                    if k not in deps or deps[k][1] < r.sig[1]:
                        deps[k] = r.sig
        for v in writes:
            for r in self._cands(v):
                k = id(r.sig[0])
                if k not in deps or deps[k][1] < r.sig[1]:
                    deps[k] = r.sig
        waits = []
        wd = self.waited[eng]
        own = self.sem_of.get(eng)
        for k, (sem, val) in deps.items():
            if eng == "pe" and sem is own:
                continue
            if wd.get(k, 0) >= val:
                continue
            wd[k] = val
            waits.append((sem, val))
        if chan is not None:
            prev = self.cnt[id(chan)]
            if prev > 0 and wd.get(id(chan), 0) < prev:
                wd[id(chan)] = prev
                waits.append((chan, prev))
            self.cnt[id(chan)] += 16
            sig = (chan, self.cnt[id(chan)])
            inc = 16
        else:
            sem = self.sem_of[eng]
            self.cnt[id(sem)] += 1
            sig = (sem, self.cnt[id(sem)])
            inc = 1
        self.ops[eng].append(Op(fn, waits, sig, inc))
        for v in writes:
            for r in self._cands(v):
                if v.lo <= r.lo and r.hi <= v.hi:
                    r.dead = True
            self._add(v, True, sig)
        for v in reads:
            for r in self._cands(v):
                if (not r.w) and r.sig[0] is sig[0] and v.lo <= r.lo and r.hi <= v.hi and chan is None:
                    r.dead = True
            self._add(v, False, sig)
        return sig

    def wait_sig(self, eng, sig):
        self.ops[eng].append(Op(None, [sig], None, 0))

    def finish(self):
        kb = self

        def run(engname):
            def body(eng):
                for op in kb.ops[engname]:
                    for (sem, val) in op.waits:
                        eng.wait_ge(sem, val)
                    if op.fn is None:
                        continue
                    ins = op.fn(eng)
                    ins.then_inc(op.sig[0], op.inc)
            return body

        with self.nc.Block() as block:
            block.tensor(run("pe"))
            block.scalar(run("act"))
            block.vector(run("dve"))
            block.gpsimd(run("pool"))
            block.sync(run("sp"))


def _aps(x):
    return x.ap if isinstance(x, V) else x


class Em:
    def __init__(self, kb):
        self.kb = kb

    def mm(self, out_v, terms):
        n = len(terms)

        def fn(e):
            ins = None
            for i, (l, r, o) in enumerate(terms):
                ins = e.matmul((o or out_v).ap, l.ap, r.ap, start=(i == 0), stop=(i == n - 1))
            return ins
        reads = [t[0] for t in terms] + [t[1] for t in terms]
        return self.kb.emit("pe", fn, reads=reads, writes=[out_v])

    def transpose(self, out_v, in_v, ident_v):
        return self.kb.emit("pe", lambda e: e.transpose(out=out_v.ap, in_=in_v.ap, identity=ident_v.ap),
                            reads=[in_v, ident_v], writes=[out_v])

    def act(self, out_v, in_v, func, scale=1.0, bias=None, rd=None, wr=None):
        reads = list(rd) if rd is not None else [in_v]
        kw = {}
        if isinstance(scale, V):
            reads.append(scale)
        if bias is not None:
            kw["bias"] = _aps(bias)
            if isinstance(bias, V):
                reads.append(bias)
        return self.kb.emit("act", lambda e: e.activation(out=out_v.ap, in_=in_v.ap, func=func, scale=_aps(scale), **kw),
                            reads=reads, writes=list(wr) if wr is not None else [out_v])

    def tt(self, eng, out_v, a, b, op, rd=None, wr=None):
        return self.kb.emit(eng, lambda e: e.tensor_tensor(out=out_v.ap, in0=a.ap, in1=b.ap, op=op),
                            reads=list(rd) if rd is not None else [a, b], writes=list(wr) if wr is not None else [out_v])

    def ts(self, eng, out_v, a, s1, op0, s2=None, op1=None, rd=None, wr=None):
        reads = list(rd) if rd is not None else [a]
        for s in (s1, s2):
            if isinstance(s, V):
                reads.append(s)
        kw = {}
        if op1 is not None:
            kw["op1"] = op1
        return self.kb.emit(eng, lambda e: e.tensor_scalar(out=out_v.ap, in0=a.ap, scalar1=_aps(s1), scalar2=_aps(s2), op0=op0, **kw),
                            reads=reads, writes=list(wr) if wr is not None else [out_v])

    def stt(self, out_v, a, s, b, op0, op1, rd=None, wr=None):
        reads = list(rd) if rd is not None else [a, b]
        if isinstance(s, V):
            reads.append(s)
        return self.kb.emit("dve", lambda e: e.scalar_tensor_tensor(out=out_v.ap, in0=a.ap, scalar=_aps(s), in1=b.ap, op0=op0, op1=op1),
                            reads=reads, writes=list(wr) if wr is not None else [out_v])

    def copy(self, eng, out_v, in_v, rd=None, wr=None):
        if eng == "act":
            f = lambda e: e.copy(out=out_v.ap, in_=in_v.ap)
        else:
            f = lambda e: e.tensor_copy(out=out_v.ap, in_=in_v.ap)
        return self.kb.emit(eng, f, reads=list(rd) if rd is not None else [in_v], writes=list(wr) if wr is not None else [out_v])

    def recip(self, out_v, in_v):
        return self.kb.emit("dve", lambda e: e.reciprocal(out=out_v.ap, in_=in_v.ap), reads=[in_v], writes=[out_v])

    def memset(self, eng, out_v, val):
        return self.kb.emit(eng, lambda e: e.memset(out_v.ap, val), writes=[out_v])

    def dma(self, eng, out, in_, chan, reads=(), writes=()):
        return self.kb.emit(eng, lambda e: e.dma_start(out=_aps(out), in_=_aps(in_)), reads=reads, writes=writes, chan=chan)


C_ID, C_SW, C_ONE = 0, 128, 256
C_SGN, C_NSGN = 384, 385
C_MASK = 386
C_RT = 394
C_E = 458
C_MLO, C_MHI = 522, 523
C_N = 524
VC_PB, VC_PS, VC_D, VC_GB, VC_L1G, VC_L1B, VC_L2G, VC_L2B, VC_N = 0, 4, 8, 12, 16, 24, 32, 40, 48


def make_consts():
    c = np.zeros((128, C_N), np.float32)
    c[:, C_ID:C_ID + 128] = np.eye(128, dtype=np.float32)
    for k in range(128):
        c[k, C_SW + (k + 64) % 128] = 1.0
    c[:, C_ONE:C_ONE + 128] = 1.0
    c[:64, C_SGN] = 1.0
    c[64:, C_SGN] = -1.0
    c[:, C_NSGN] = -c[:, C_SGN]
    for q in range(8):
        c[16 * q:16 * q + 16, C_MASK + q] = 1.0
    for wi, w in enumerate(WINS):
        for t in range(16):
            c[:, C_RT + wi * 16 + t] = np.float32(1.0) / np.float32(min(t + 1, w))
    for k in range(128):
        c[k, C_E + k % 64] = 1.0
    c[:64, C_MLO] = 1.0
    c[64:, C_MHI] = 1.0
    return c


def build(nseq, nlayer):
    nc = bass.Bass("TRN2", target_bir_lowering=False)

    def din(name, shape):
        return nc.dram_tensor(name, shape, F32, kind="ExternalInput").ap()

    x_d = din("x", [nseq, SEQ, D])
    p_d = din("p", [nlayer, nseq, SEQ, 256])
    consts_d = din("consts", [128, C_N])
    vecs_d = din("vecs", [nlayer, 128, VC_N])
    ssa_d = din("ssm_a", [nlayer, 128, 96])
    ssb_d = din("ssm_b", [nlayer, 128, 1024])
    ssc_d = din("ssm_c", [nlayer, 128, 512])
    win_d = din("w_in_t", [nlayer, 8, 128, 1024])
    wout_d = din("w_out_t", [nlayer, 8, 128, 1024])
    gate_d = din("gate_t", [nlayer, 8, 128, 1024])
    w1_d = din("w1_t", [nlayer, 22, 128, 1024])
    w3_d = din("w3_t", [nlayer, 22, 128, 1024])
    w2_d = din("w2_t", [nlayer, 3, 8, 128, 1024])
    ple_d = din("ple_t", [nlayer, 8, 128, 256])
    glu_d = din("glu_t", [nlayer, 128, 2048])
    poolw_d = din("pool_t", [nlayer, 128, 512])
    out_d = nc.dram_tensor("out", [nseq, SEQ, D], F32, kind="ExternalOutput").ap()

    with contextlib.ExitStack() as st:
        kb = KB(nc, 206 * 1024, st)
        em = Em(kb)
        cst = kb.alloc(C_N, F32)
        id16 = kb.alloc(128, BF16)
        one16 = kb.alloc(128, BF16)
        vecs = kb.alloc(VC_N, F32)
        hT32 = kb.alloc(8 * SEQ, F32)
        hT16 = kb.alloc(8 * SEQ, BF16)
        mix16 = kb.alloc(8 * SEQ, BF16)
        NS = 8
        ring = [kb.alloc(1024, BF16) for _ in range(NS)]
        glu16 = kb.alloc(2048, BF16)
        poolw16 = kb.alloc(512, BF16)
        ccpad = kb.alloc(8 * 128, BF16)
        S0 = kb.sb_top
        K = 1024

        def sc(off, cols, dtype):
            return kb.at(S0 + off, cols, dtype)
        u32 = sc(0, SEQ, F32)
        u32B = sc(34 * K, SEQ, F32)
        sA = sc(8 * K, SEQ, F32)
        sB = sc(16 * K, SEQ, F32)
        z16 = [sc(24 * K, TW, BF16), sc(25 * K, TW, BF16)]
        XD = [sc(5 * K * i, SEQ, BF16) for i in range(4)]
        HB = [sc(5 * K * i + 4 * K, TW, BF16) for i in range(4)]
        mats = [sc(20 * K + 3 * K * i, 12 * 128, BF16) for i in range(2)]
        bbtpad = sc(50 * K, 8 * 128, BF16)
        ussm32 = sc(26 * K, SEQ, F32)
        ussm16 = sc(34 * K, SEQ, BF16)
        bx = sc(26 * K, 1024, F32)
        cc32 = sc(30 * K, 512, F32)
        bb32 = sc(32 * K, 512, F32)
        tmp32 = sc(34 * K, 32 * 40, F32)
        pwr = sc(42 * K, 16 * 32, F32)
        pwi = sc(44 * K, 16 * 32, F32)
        bbt = sc(46 * K, 4 * 128, BF16)
        cc16 = sc(47 * K, 512, BF16)
        bb16 = sc(48 * K, 512, BF16)
        ssa = sc(49 * K, 96, F32)
        lx16 = sc(0, 8 * TW, BF16)
        lsq16 = sc(8 * K, 8 * TW, BF16)
        lmean = sc(16 * K, TW, F32)
        lt1 = sc(18 * K, TW, F32)
        lrstd = sc(20 * K, TW, F32)
        pstage = sc(0, 16 * 256, F32)
        pT16 = sc(16 * K, 2 * SEQ, BF16)
        fA = [sc(24 * K + 2 * K * i, TW, F32) for i in range(4)]
        xstage = [sc(0, D, F32), sc(4 * K, D, F32)]
        assert S0 + 50 * K <= 206 * K, S0

        ident32 = cst.v(C_ID, C_ID + 128)
        swap32 = cst.v(C_SW, C_SW + 128)

        ch_c = kb.chan("c")
        ch_ring = [kb.chan("r") for _ in range(NS)]
        ch_misc = [kb.chan("m") for _ in range(6)]
        ch_xin = [kb.chan("xi") for _ in range(2)]
        ch_out = [kb.chan("xo") for _ in range(2)]
        ch_p = kb.chan("p")

        em.dma("sp", cst.v(), consts_d, ch_c, writes=[cst.v()])
        em.copy("dve", id16.v(), ident32)
        em.copy("dve", one16.v(), cst.v(C_ONE, C_ONE + 128))
        if "m" not in KSKIP:
            em.memset("pool", ccpad.v(), 0.0)

        chunks = []
        for s in range(nseq):
            for l in range(nlayer):
                for m in range(8):
                    chunks.append((win_d[l, m], 8))
                for m in range(8):
                    chunks.append((wout_d[l, m], 8))
                for pi, (f0, fn_) in enumerate(FPASS):
                    for f in range(f0, f0 + fn_):
                        chunks.append((w1_d[l, f], 8))
                        chunks.append((w3_d[l, f], 8))
                    for m in range(8):
                        chunks.append((w2_d[l, pi, m], fn_))
                for m in range(8):
                    chunks.append((gate_d[l, m], 8))
                    chunks.append((ple_d[l, m], 2))
        wstate = {"issued": 0, "next": 0}

        def w_issue(upto):
            while wstate["issued"] < min(upto, len(chunks)):
                i = wstate["issued"]
                src, kt = chunks[i]
                slot = ring[i % NS]
                dst = slot.v(0, kt * 128)
                em.dma("pool", dst, src[:, 0:kt * 128], ch_ring[i % NS], writes=[dst])
                wstate["issued"] += 1

        def w_next(kt):
            i = wstate["next"]
            assert chunks[i][1] == kt, (i, chunks[i][1], kt)
            w_issue(i + NS - 1)
            wstate["next"] += 1
            return ring[i % NS]

        if PH >= 0:
            w_issue(NS - 2)

        def load_x(s):
            for tt in range(16):
                stg = xstage[tt % 2]
                em.dma("sp", stg.v(), x_d[s, tt * 128:(tt + 1) * 128, :], ch_xin[tt % 2], writes=[stg.v()])
                for half in range(2):
                    bk = kb.nb()
                    for j in range(4):
                        m = half * 4 + j
                        em.transpose(bk.v(j * 128, j * 128 + 128), stg.v(m * 128, m * 128 + 128), ident32)
                    o32 = hT32.v3((half * 4) * SEQ + tt * 128, 4, SEQ, 128)
                    o16 = hT16.v3((half * 4) * SEQ + tt * 128, 4, SEQ, 128)
                    src = bk.v3(0, 4, 128, 128)
                    wr32 = hT32.parts((half * 4) * SEQ + tt * 128, 4, SEQ, 128)
                    wr16 = hT16.parts((half * 4) * SEQ + tt * 128, 4, SEQ, 128)
                    em.copy("act", o32, src, rd=[bk.v()], wr=wr32)
                    em.copy("dve", o16, src, rd=[bk.v()], wr=wr16)

        def store_out(s):
            for tt in range(16):
                stg = xstage[tt % 2]
                for half in range(2):
                    bk = kb.nb()
                    for j in range(4):
                        m = half * 4 + j
                        em.transpose(bk.v(j * 128, j * 128 + 128), hT32.v(m * SEQ + tt * 128, m * SEQ + tt * 128 + 128), ident32)
                    eng = "act" if half == 0 else "dve"
                    em.copy(eng, stg.v(half * 512, half * 512 + 512), bk.v())
                sig = em.dma("sp", out_d[s, tt * 128:(tt + 1) * 128, :], stg.v(), ch_out[tt % 2], reads=[stg.v()])
                final_sigs[tt % 2] = sig

        final_sigs = [None, None]

        def ssm_prep(l, stg):
            bx = sc(stg, 1024, F32)
            cc32 = sc(stg + 4 * K, 512, F32)
            bb32 = sc(stg + 6 * K, 512, F32)
            tmp32 = sc(stg + 8 * K, 32 * 40, F32)
            em.dma("sp", ssa.v(), ssa_d[l], ch_misc[0], writes=[ssa.v()])
            em.dma("sp", bx.v(), ssb_d[l], ch_misc[1], writes=[bx.v()])
            em.dma("sp", cc32.v(), ssc_d[l], ch_misc[2], writes=[cc32.v()])
            T = [tmp32.v(32 * i, 32 * i + 32) for i in range(40)]
            are, aim, ldt = ssa.v(0, 32), ssa.v(32, 64), ssa.v(64, 96)
            arec, dt, tre, er, ang = T[0], T[1], T[2], T[3], T[4]
            em.ts("dve", arec, are, -1e-4, ALU.min)
            em.act(dt, ldt, AF.Exp)
            em.tt("dve", tre, arec, dt, ALU.mult)
            em.act(er, tre, AF.Exp)
            em.tt("dve", ang, aim, dt, ALU.mult)
            ki = kb.at(tmp32.base + 32 * 4 * 39, 32, I32)

            def sin_of(dst, src, shift, t0, t1):
                em.ts("dve", t0, src, shift, ALU.add)
                em.ts("dve", ki.v(), t0, 1.0 / (2 * math.pi), ALU.mult)
                em.copy("dve", t1, ki.v())
                em.stt(t1, t1, -2.0 * math.pi, t0, ALU.mult, ALU.add)
                em.ts("dve", t0, t1, math.pi, ALU.is_gt, -2.0 * math.pi, ALU.mult)
                em.tt("dve", t1, t1, t0, ALU.add)
                em.ts("dve", t0, t1, -math.pi, ALU.is_lt, 2.0 * math.pi, ALU.mult)
                em.tt("dve", t1, t1, t0, ALU.add)
                em.ts("dve", t1, t1, 3.1415925, ALU.min, -3.1415925, ALU.max)
                em.act(dst, t1, AF.Sin)
            sinv, cosv = T[5], T[6]
            sin_of(sinv, ang, 0.0, T[7], T[8])
            sin_of(cosv, ang, math.pi / 2, T[7], T[8])
            lr, li = T[9], T[10]
            em.tt("dve", lr, er, cosv, ALU.mult)
            em.tt("dve", li, er, sinv, ALU.mult)
            xm1, den, rden, a1, a2, cr, ci = T[11], T[12], T[13], T[14], T[15], T[16], T[17]
            em.ts("dve", xm1, lr, -1.0, ALU.add)
            em.tt("dve", den, arec, arec, ALU.mult)
            em.tt("dve", a1, aim, aim, ALU.mult)
            em.tt("dve", den, den, a1, ALU.add)
            em.recip(rden, den)
            em.tt("dve", a1, xm1, arec, ALU.mult)
            em.tt("dve", a2, li, aim, ALU.mult)
            em.tt("dve", a1, a1, a2, ALU.add)
            em.tt("dve", cr, a1, rden, ALU.mult)
            em.tt("dve", a1, li, arec, ALU.mult)
            em.tt("dve", a2, xm1, aim, ALU.mult)
            em.tt("dve", a1, a1, a2, ALU.subtract)
            em.tt("dve", ci, a1, rden, ALU.mult)
            c1, c2 = T[18], T[19]
            em.ts("dve", c1, cr, cst.v(C_SGN, C_SGN + 1), ALU.mult)
            em.ts("dve", c2, ci, -1.0, ALU.mult)
            c1b = V(c1.ap.unsqueeze(2).broadcast_to([128, 32, 16]), "sb", c1.lo, c1.hi)
            c2b = V(c2.ap.unsqueeze(2).broadcast_to([128, 32, 16]), "sb", c2.lo, c2.hi)
            bx1 = bx.v3(0, 32, 16, 16)
            bx2 = bx.v3(512, 32, 16, 16)
            em.tt("dve", bb32.v3(0, 32, 16, 16), bx1, c1b, ALU.mult)
            em.tt("dve", bx2, bx2, c2b, ALU.mult)
            em.tt("dve", bb16.v(), bb32.v(), bx.v(512, 1024), ALU.add)
            em.copy("dve", cc16.v(), cc32.v())
            bk = kb.nb(BF16)
            for t in range(4):
                em.transpose(bk.v(t * 128, t * 128 + 128), bb16.v(t * 128, t * 128 + 128), id16.v())
            em.copy("dve", bbt.v(), bk.v(0, 512))
            pw = {1: (lr, li)}
            nxt = [20]

            def newt():
                i = nxt[0]
                nxt[0] += 1
                return T[i]

            def csq(a):
                re, im = newt(), newt()
                em.tt("dve", T[38], a[1], a[1], ALU.mult)
                em.tt("dve", re, a[0], a[0], ALU.mult)
                em.tt("dve", re, re, T[38], ALU.subtract)
                em.stt(im, a[0], 2.0, a[1], ALU.mult, ALU.mult)
                return (re, im)

            def cmul(a, b, re, im):
                em.tt("dve", T[38], a[1], b[1], ALU.mult)
                em.tt("dve", re, a[0], b[0], ALU.mult)
                em.tt("dve", re, re, T[38], ALU.subtract)
                em.tt("dve", T[38], a[1], b[0], ALU.mult)
                em.tt("dve", im, a[0], b[1], ALU.mult)
                em.tt("dve", im, im, T[38], ALU.add)

            mlo, mhi = cst.v(C_MLO, C_MLO + 1), cst.v(C_MHI, C_MHI + 1)

            def store(p, a):
                i = PWIDX[p]
                c0, c1 = pwr.v(32 * i, 32 * i + 32), pwi.v(32 * i, 32 * i + 32)
                em.ts("dve", T[36], a[1], cst.v(C_NSGN, C_NSGN + 1), ALU.mult)
                em.ts("dve", T[37], T[36], mhi, ALU.mult)
                em.stt(c0, a[0], mlo, T[37], ALU.mult, ALU.add)
                em.ts("dve", T[37], a[0], mhi, ALU.mult)
                em.stt(c1, T[36], mlo, T[37], ALU.mult, ALU.add)
            cur = pw[1]
            store(1, cur)
            e = 1
            while e < 1024:
                nxt[0] = 20 + (int(math.log2(e)) % 2) * 6
                sq = csq(cur)
                store(2 * e, sq) if (2 * e) in PWIDX else None
                if (3 * e) in PWIDX:
                    t3 = (newt(), newt())
                    cmul(sq, cur, t3[0], t3[1])
                    store(3 * e, t3)
                cur = sq
                e *= 2

        def gen_mats(stage_i, g0, buf):
            s, r = STAGES[stage_i]
            e32 = cst.v(C_E, C_E + 64)
            eb = V(e32.ap.unsqueeze(1).broadcast_to([128, 4, 64]), "sb", e32.lo, e32.hi)
            for k in range(1, r):
                pi = PWIDX[s * k]
                for h, src in ((0, pwr), (1, pwi)):
                    a = src.v(32 * pi + g0, 32 * pi + g0 + 4)
                    ab = V(a.ap.unsqueeze(2).broadcast_to([128, 4, 64]), "sb", a.lo, a.hi)
                    c0 = (k - 1) * 128 + h * 64
                    o = buf.v3(c0, 4, 384, 64)
                    em.tt("pool", o, eb, ab, ALU.mult, wr=buf.parts(c0, 4, 384, 64))

        mats0 = sc(38 * K, 12 * 128, BF16)

        def ssm_tile(l, mt, evq):
            for q in range(8):
                em.ts("pool", bbtpad.v(q * 128, q * 128 + 128), bbt.v(mt * 128, mt * 128 + 128), cst.v(C_MASK + q, C_MASK + q + 1), ALU.mult)
            for q in range(8):
                g = mt * 8 + q
                em.copy("pool", ccpad.v(q * 128 + q * 16, q * 128 + q * 16 + 16), cc16.v(g * 16, g * 16 + 16))
            dcol = vecs.v(VC_D + mt, VC_D + mt + 1)
            def evac(dst, bk):
                ev = evq[0] % 2
                evq[0] += 1
                em.copy("act" if ev == 0 else "dve", dst, bk.v())

            def evac_add(dst, bk, addsrc, c0=0):
                em.tt("dve", dst, bk.v(c0, TW), addsrc, ALU.add)

            for b in range(2):
                g0 = mt * 8 + b * 4
                gen_mats(0, g0, mats0)

                def M(buf, gi, k):
                    return buf.v(gi * 384 + (k - 1) * 128, gi * 384 + k * 128)
                for gi in range(4):
                    q = b * 4 + gi
                    for r in range(4):
                        bk = kb.nb()
                        em.mm(bk.v(), [(bbtpad.v(q * 128, q * 128 + 128), ussm16.v(r * TW, r * TW + TW), None)])
                        em.copy("act", XD[gi].v(r * TW, r * TW + TW), bk.v())
                gen_mats(1, g0, mats[1])
                for gi in range(4):
                    bk = kb.nb()
                    fold = (gi % 2 == 0)
                    terms = [] if fold else [(id16.v(), XD[gi].v(3 * TW, 4 * TW), None)]
                    for k in range(1, 4):
                        terms.append((M(mats0, gi, k), XD[gi].v((3 - k) * TW, (4 - k) * TW), None))
                    em.mm(bk.v(), terms)
                    if fold:
                        evac_add(HB[gi].v(), bk, XD[gi].v(3 * TW, 4 * TW))
                    else:
                        em.copy("act", HB[gi].v(), bk.v())
                for si in range(1, len(STAGES)):
                    s_, r_ = STAGES[si]
                    mb = mats[si % 2]
                    if si + 1 < len(STAGES):
                        gen_mats(si + 1, g0, mats[(si + 1) % 2])
                    for gi in range(4):
                        bk = kb.nb()
                        sh1 = s_ // 4
                        fold = (gi % 2 == 0)
                        terms = [] if fold else [(id16.v(), HB[gi].v(), None)]
                        for k in range(1, r_):
                            sh = s_ * k // 4
                            terms.append((M(mb, gi, k), HB[gi].v(0, TW - sh), bk.v(sh, TW)))
                        if fold:
                            em.mm(bk.v(sh1, TW), terms)
                            evac_add(HB[gi].v(sh1, TW), bk, HB[gi].v(sh1, TW), sh1)
                        else:
                            em.mm(bk.v(), terms)
                            em.copy("act", HB[gi].v(), bk.v())
                m0 = mats0
                for gi in range(4):
                    for r in (2, 1, 0):
                        bk = kb.nb()
                        fold = (gi % 2 == 0)
                        terms = [] if fold else [(id16.v(), XD[gi].v(r * TW, (r + 1) * TW), None)]
                        for k in range(1, r + 1):
                            terms.append((M(m0, gi, k), XD[gi].v((r - k) * TW, (r - k + 1) * TW), None))
                        terms.append((M(m0, gi, r + 1), HB[gi].v(0, TW - 1), bk.v(1, TW)))
                        if not fold:
                            em.mm(bk.v(), terms)
                            em.copy("act", XD[gi].v((r + 1) * TW, (r + 2) * TW), bk.v())
                        elif r == 0:
                            em.mm(bk.v(1, TW), terms)
                            evac_add(XD[gi].v(TW + 1, 2 * TW), bk, XD[gi].v(1, TW), 1)
                            em.copy("act", XD[gi].v(TW, TW + 1), XD[gi].v(0, 1))
                        else:
                            em.mm(bk.v(), terms)
                            evac_add(XD[gi].v((r + 1) * TW, (r + 2) * TW), bk, XD[gi].v(r * TW, (r + 1) * TW))
                for r in range(4):
                    bk = kb.nb()
                    terms = []
                    for gi in range(4):
                        src = HB[gi].v() if r == 3 else XD[gi].v((r + 1) * TW, (r + 2) * TW)
                        terms.append((ccpad.v((b * 4 + gi) * 128, (b * 4 + gi) * 128 + 128), src, None))
                    em.mm(bk.v(), terms)
                    u_ap = ussm32.full.rearrange("p (j r) -> p r j", r=4)[:, r, :]
                    uv = V(u_ap, "sb", ussm32.base, ussm32.base + 4 * SEQ)
                    if b == 0:
                        em.stt(uv, uv, dcol, bk.v(), ALU.mult, ALU.add)
                    else:
                        em.tt("dve", uv, uv, bk.v(), ALU.add)
            for n in range(NT):
                em.act(mix16.v((4 + mt) * SEQ + n * TW, (4 + mt) * SEQ + n * TW + TW), ussm32.v(n * TW, n * TW + TW), AF.Gelu_apprx_tanh)

        def pool_tile(l, g, u32):
            w = WINS[g]
            src = u32
            bufs = [sA, sB]
            bi = 0
            step = 1
            while step < w:
                dst = bufs[bi]
                em.tt("dve", dst.v(step, SEQ), src.v(step, SEQ), src.v(0, SEQ - step), ALU.add)
                em.copy("pool", dst.v(0, step), src.v(0, step))
                src = dst
                bi ^= 1
                step *= 2
            zb = bufs[bi]
            em.stt(zb.v(), src.v(), 1.0 / w, u32.v(), ALU.mult, ALU.subtract)
            em.tt("dve", zb.v(0, 16), src.v(0, 16), cst.v(C_RT + g * 16, C_RT + g * 16 + 16), ALU.mult)
            em.tt("dve", zb.v(0, 16), zb.v(0, 16), u32.v(0, 16), ALU.subtract)
            for n in range(NT):
                zz = z16[n % 2]
                em.copy("act", zz.v(), zb.v(n * TW, n * TW + TW))
                bk = kb.nb()
                em.mm(bk.v(), [(poolw16.v(g * 128, g * 128 + 128), zz.v(), None)])
                em.ts("dve", mix16.v(g * SEQ + n * TW, g * SEQ + n * TW + TW), bk.v(),
                      vecs.v(VC_PB + g, VC_PB + g + 1), ALU.add, vecs.v(VC_PS + g, VC_PS + g + 1), ALU.mult)

        LX = [sc(0, 8 * TW, BF16), sc(16 * K, 8 * TW, BF16)]
        LSQ = [sc(8 * K, 8 * TW, BF16), sc(24 * K, 8 * TW, BF16)]
        LMEAN = [sc(32 * K, TW, F32), sc(34 * K, TW, F32)]
        LRSTD = [sc(36 * K, TW, F32), sc(38 * K, TW, F32)]
        LT1 = sc(40 * K, TW, F32)

        def layer_norm(gcol, bcol):
            def A(n):
                c0 = n * TW
                xin = hT32.v3(c0, 8, SEQ, TW)
                xparts = hT32.parts(c0, 8, SEQ, TW)
                lx16, lsq16, lmean, lrstd = LX[n % 2], LSQ[n % 2], LMEAN[n % 2], LRSTD[n % 2]
                em.act(lsq16.v3(0, 8, TW, TW), xin, AF.Square, rd=xparts, wr=[lsq16.v()])
                em.copy("pool", lx16.v3(0, 8, TW, TW), xin, rd=xparts, wr=[lx16.v()])
                bs = kb.nb()
                bq = kb.nb()
                em.mm(bs.v(), [(one16.v(), lx16.v(m * TW, m * TW + TW), None) for m in range(8)])
                em.mm(bq.v(), [(one16.v(), lsq16.v(m * TW, m * TW + TW), None) for m in range(8)])
                em.ts("dve", lmean.v(), bs.v(), 1.0 / D, ALU.mult)
                em.tt("dve", LT1.v(), lmean.v(), lmean.v(), ALU.mult)
                em.stt(LT1.v(), bq.v(), 1.0 / D, LT1.v(), ALU.mult, ALU.subtract)
                em.act(LT1.v(), LT1.v(), AF.Sqrt, bias=LN_EPS)
                em.recip(lrstd.v(), LT1.v())

            def B(n):
                c0 = n * TW
                xin = hT32.v3(c0, 8, SEQ, TW)
                xparts = hT32.parts(c0, 8, SEQ, TW)
                lmean, lrstd = LMEAN[n % 2], LRSTD[n % 2]
                mb_ = V(lmean.v().ap.unsqueeze(1).broadcast_to([128, 8, TW]), "sb", lmean.base, lmean.base + 2048)
                rb_ = V(lrstd.v().ap.unsqueeze(1).broadcast_to([128, 8, TW]), "sb", lrstd.base, lrstd.base + 2048)
                em.tt("pool", xin, xin, mb_, ALU.subtract, rd=xparts + [lmean.v()], wr=xparts)
                em.tt("dve", xin, xin, rb_, ALU.mult, rd=xparts + [lrstd.v()], wr=xparts)
                for m in range(8):
                    hv = hT32.v(m * SEQ + c0, m * SEQ + c0 + TW)
                    em.act(hv, hv, AF.Identity, scale=vecs.v(gcol + m, gcol + m + 1), bias=vecs.v(bcol + m, bcol + m + 1))
                em.copy("act", hT16.v3(c0, 8, SEQ, TW), xin, rd=xparts, wr=hT16.parts(c0, 8, SEQ, TW))
            A(0)
            A(1)
            B(0)
            A(2)
            B(1)
            A(3)
            B(2)
            B(3)

        def layer(s, l, first, next_prep):
            kb.new_epoch()
            em.dma("sp", vecs.v(), vecs_d[l], ch_misc[3], writes=[vecs.v()])
            em.dma("pool", glu16.v(), glu_d[l], ch_misc[4], writes=[glu16.v()])
            em.dma("pool", poolw16.v(), poolw_d[l], ch_misc[5], writes=[poolw16.v()])
            if PH < 1:
                return
            if first:
                ssm_prep(l, 26 * K)
            evq = [0]
            if PH < 2:
                return
            for m in range(8):
                wc = w_next(8)
                for n in range(NT):
                    bk = kb.nb()
                    em.mm(bk.v(), [(wc.v(k * 128, k * 128 + 128), hT16.v(k * SEQ + n * TW, k * SEQ + n * TW + TW), None) for k in range(8)])
                    if m < 4:
                        em.copy("act", u32.v(n * TW, n * TW + TW), bk.v())
                    else:
                        em.copy("act", ussm32.v(n * TW, n * TW + TW), bk.v())
                        o_ap = ussm16.full.rearrange("p (r j) -> p r j", r=4)[:, :, n * 128:(n + 1) * 128]
                        i_ap = bk.full.rearrange("p (j r) -> p r j", r=4)
                        em.copy("dve", V(o_ap, "sb", ussm16.base, ussm16.base + 4096), V(i_ap, "ps", bk.base, bk.base + 2048))
                if m < 4:
                    pool_tile(l, m, u32)
                elif PH >= 3:
                    ssm_tile(l, m - 4, evq)
            if PH < 4:
                return
            for n in range(NT):
                bks = []
                for m in range(4):
                    bk = kb.nb()
                    em.mm(bk.v(), [(glu16.v(k * 512 + m * 128, k * 512 + m * 128 + 128),
                                    mix16.v((4 + k) * SEQ + n * TW, (4 + k) * SEQ + n * TW + TW), None) for k in range(4)])
                    bks.append(bk)
                for m in range(4):
                    sg = fA[m % 4]
                    em.act(sg.v(), bks[m].v(), AF.Sigmoid, bias=vecs.v(VC_GB + m, VC_GB + m + 1))
                    mv = mix16.v((4 + m) * SEQ + n * TW, (4 + m) * SEQ + n * TW + TW)
                    em.tt("dve", mv, mv, sg.v(), ALU.mult)
            for m in range(8):
                wc = w_next(8)
                for n in range(NT):
                    bk = kb.nb()
                    em.mm(bk.v(), [(wc.v(k * 128, k * 128 + 128), mix16.v(k * SEQ + n * TW, k * SEQ + n * TW + TW), None) for k in range(8)])
                    hv = hT32.v(m * SEQ + n * TW, m * SEQ + n * TW + TW)
                    em.stt(hv, hv, ALPHA, bk.v(), ALU.mult, ALU.add)
            if PH < 5:
                return
            layer_norm(VC_L1G, VC_L1B)
            if PH < 6:
                return
            em.dma("sp", pstage.v3(0, 16, 256, 256), p_d[l, s].rearrange("(t q) c -> q t c", q=128), ch_p, writes=[pstage.v()])
            for kt in range(2):
                for quarter in range(4):
                    bk = kb.nb()
                    for j in range(4):
                        tt = quarter * 4 + j
                        em.transpose(bk.v(j * 128, j * 128 + 128), pstage.v(tt * 256 + kt * 128, tt * 256 + kt * 128 + 128), ident32)
                    em.copy("act", pT16.v(kt * SEQ + quarter * 512, kt * SEQ + quarter * 512 + 512), bk.v())
            if PH < 7:
                return
            for pi, (f0, fn_) in enumerate(FPASS):
                for fl in range(fn_):
                    w1c = w_next(8)
                    w3c = w_next(8)
                    for n in range(NT):
                        ba = kb.nb()
                        bb_ = kb.nb()
                        em.mm(ba.v(), [(w1c.v(k * 128, k * 128 + 128), hT16.v(k * SEQ + n * TW, k * SEQ + n * TW + TW), None) for k in range(8)])
                        em.mm(bb_.v(), [(w3c.v(k * 128, k * 128 + 128), hT16.v(k * SEQ + n * TW, k * SEQ + n * TW + TW), None) for k in range(8)])
                        sa_ = fA[(fl * NT + n) % 4]
                        em.act(sa_.v(), ba.v(), AF.Silu)
                        em.tt("dve", mix16.v(fl * SEQ + n * TW, fl * SEQ + n * TW + TW), sa_.v(), bb_.v(), ALU.mult)
                for m in range(8):
                    w2c = w_next(fn_)
                    for n in range(NT):
                        bk = kb.nb()
                        em.mm(bk.v(), [(w2c.v(k * 128, k * 128 + 128), mix16.v(k * SEQ + n * TW, k * SEQ + n * TW + TW), None) for k in range(fn_)])
                        hv = hT32.v(m * SEQ + n * TW, m * SEQ + n * TW + TW)
                        if pi == 0:
                            em.stt(hv, hv, ALPHA, bk.v(), ALU.mult, ALU.add)
                        else:
                            em.tt("dve", hv, hv, bk.v(), ALU.add)
                        if pi == len(FPASS) - 1:
                            em.copy("act", hT16.v(m * SEQ + n * TW, m * SEQ + n * TW + TW), hv)
                if pi == 0 and next_prep is not None:
                    ssm_prep(next_prep, 0)
            for m in range(8):
                wg = w_next(8)
                wp = w_next(2)
                for n in range(NT):
                    bg = kb.nb()
                    be = kb.nb()
                    em.mm(bg.v(), [(wg.v(k * 128, k * 128 + 128), hT16.v(k * SEQ + n * TW, k * SEQ + n * TW + TW), None) for k in range(8)])
                    em.mm(be.v(), [(wp.v(k * 128, k * 128 + 128), pT16.v(k * SEQ + n * TW, k * SEQ + n * TW + TW), None) for k in range(2)])
                    sg = fA[(m * NT + n) % 4]
                    em.act(sg.v(), bg.v(), AF.Sigmoid)
                    em.tt("dve", sg.v(), sg.v(), be.v(), ALU.mult)
                    hv = hT32.v(m * SEQ + n * TW, m * SEQ + n * TW + TW)
                    em.tt("dve", hv, hv, sg.v(), ALU.add)
            layer_norm(VC_L2G, VC_L2B)

        for s in range(nseq):
            if "x" in KSKIP:
                continue
            load_x(s)
            for l in range(nlayer):
                if PH >= 0:
                    idx = s * nlayer + l
                    layer(s, l, idx == 0, ((idx + 1) % nlayer) if idx + 1 < nseq * nlayer else None)
            store_out(s)
        for sig in final_sigs:
            if sig is not None:
                kb.wait_sig("sp", sig)
        kb.finish()
    return nc


def _tile_w(W):
    Kd, Md = W.shape
    return np.ascontiguousarray(W.reshape(Kd // 128, 128, Md // 128, 128).transpose(2, 1, 0, 3)).reshape(Md // 128, 128, Kd)


def prep_weights(inp, layers):
    f = lambda a: np.asarray(a, np.float32)
    L = len(layers)
    o = {}
    o["consts"] = make_consts()
    vec = np.zeros((L, 128, VC_N), np.float32)
    ssa = np.zeros((L, 128, 96), np.float32)
    ssb = np.zeros((L, 128, 1024), np.float32)
    ssc = np.zeros((L, 128, 512), np.float32)
    for i, l in enumerate(layers):
        def colize(v, ntile):
            return f(v).reshape(ntile, 128).T
        vec[i, :, VC_PB:VC_PB + 4] = colize(inp["pool_b"][l], 4)
        vec[i, :, VC_PS:VC_PS + 4] = colize(inp["pool_scale"][l], 4)
        vec[i, :, VC_D:VC_D + 4] = colize(inp["ssm_d"][l], 4)
        vec[i, :, VC_GB:VC_GB + 4] = colize(inp["ssm_glu_b"][l], 4)
        vec[i, :, VC_L1G:VC_L1G + 8] = colize(inp["ln1_g"][l], 8)
        vec[i, :, VC_L1B:VC_L1B + 8] = colize(inp["ln1_b"][l], 8)
        vec[i, :, VC_L2G:VC_L2G + 8] = colize(inp["ln2_g"][l], 8)
        vec[i, :, VC_L2B:VC_L2B + 8] = colize(inp["ln2_b"][l], 8)
        are = f(inp["ssm_a_re"][l]).T
        aim = f(inp["ssm_a_im"][l]).T
        ssa[i, :, 0:32] = np.concatenate([are, are], 0)
        ssa[i, :, 32:64] = np.concatenate([aim, aim], 0)
        ssa[i, :, 64:96] = np.broadcast_to(f(inp["ssm_log_dt"][l])[None, :], (128, 32))
        bre = f(inp["ssm_b_re"][l]).transpose(1, 0, 2).reshape(64, 512)
        bim = f(inp["ssm_b_im"][l]).transpose(1, 0, 2).reshape(64, 512)
        ssb[i, :, 0:512] = np.concatenate([bre, bim], 0)
        ssb[i, :, 512:1024] = np.concatenate([bim, bre], 0)
        cre = f(inp["ssm_c_re"][l]).transpose(2, 0, 1).reshape(64, 512)
        cim = f(inp["ssm_c_im"][l]).transpose(2, 0, 1).reshape(64, 512)
        ssc[i] = np.concatenate([cre, cim], 0)
    o["vecs"], o["ssm_a"], o["ssm_b"], o["ssm_c"] = vec, ssa, ssb, ssc
    o["w_in_t"] = np.stack([_tile_w(f(inp["w_in"][l])) for l in layers])
    o["w_out_t"] = np.stack([_tile_w(f(inp["w_out"][l])) for l in layers])
    o["gate_t"] = np.stack([_tile_w(f(inp["ple_gate_w"][l])) for l in layers])
    o["w1_t"] = np.stack([_tile_w(f(inp["ffn_w1"][l])) for l in layers])
    o["w3_t"] = np.stack([_tile_w(f(inp["ffn_w3"][l])) for l in layers])
    w2 = np.zeros((L, 3, 8, 128, 1024), np.float32)
    for i, l in enumerate(layers):
        W2 = f(inp["ffn_w2"][l])
        for pi, (f0, fn_) in enumerate(FPASS):
            w2[i, pi, :, :, :fn_ * 128] = _tile_w(W2[f0 * 128:(f0 + fn_) * 128, :])
    o["w2_t"] = w2
    o["ple_t"] = np.stack([_tile_w(f(inp["ple_w"][l])) for l in layers])
    o["glu_t"] = np.stack([np.ascontiguousarray(f(inp["ssm_glu_w"][l]).reshape(4, 128, 512).transpose(1, 0, 2)).reshape(128, 2048) for l in layers])
    o["pool_t"] = np.stack([np.ascontiguousarray(f(inp["pool_w"][l]).transpose(1, 0, 2)).reshape(128, 512) for l in layers])
    return o


_NC_CACHE = {}


def kernel(**inputs):
    x = np.asarray(inputs["x"], np.float32)
    p = np.asarray(inputs["p"], np.float32)
    B = x.shape[0]
    nseq = B // NCORES
    w = prep_weights(inputs, list(range(DEPTH)))
    key = (nseq, DEPTH)
    if key not in _NC_CACHE:
        _NC_CACHE[key] = build(nseq, DEPTH)
    nc = _NC_CACHE[key]
    in_maps = []
    for c in range(NCORES):
        d = dict(w)
        d["x"] = np.ascontiguousarray(x[c * nseq:(c + 1) * nseq])
        d["p"] = np.ascontiguousarray(p[:, c * nseq:(c + 1) * nseq])
        in_maps.append(d)
    res = run_bass_kernel_spmd(nc, in_maps, core_ids=list(range(NCORES)))
    return np.concatenate([r["out"] for r in res.results], axis=0)
```

```python
import contextlib
import math
import os
PH = int(os.environ.get('KPH', '99'))
KSKIP = os.environ.get('KSKIP', '')
import numpy as np
import concourse.bass as bass
import concourse.mybir as mybir
from concourse.bass_utils import run_bass_kernel_spmd

F32 = mybir.dt.float32
BF16 = mybir.dt.bfloat16
I32 = mybir.dt.int32
AF = mybir.ActivationFunctionType
ALU = mybir.AluOpType
ESZ = {F32: 4, BF16: 2, I32: 4}
ENGS = ("pe", "act", "dve", "pool", "sp")
PAGE = 128

NCORES = 8
DEPTH = 4
SEQ = 2048
D = 1024
DFF = 2816
NT = 4
TW = 512
ALPHA = (2.0 * DEPTH) ** 0.25
LN_EPS = 1e-5
WINS = (2, 4, 8, 16)
STAGES = [(1, 4), (4, 4), (16, 4), (64, 4), (256, 4), (1024, 2)]
PWLIST = []
for _s, _r in STAGES:
    for _k in range(1, _r):
        PWLIST.append(_s * _k)
PWIDX = {p: i for i, p in enumerate(PWLIST)}
FPASS = [(0, 8), (8, 8), (16, 6)]


class V:
    __slots__ = ("ap", "space", "lo", "hi")

    def __init__(self, ap, space, lo, hi):
        self.ap, self.space, self.lo, self.hi = ap, space, lo, hi


class Buf:
    def __init__(self, full_ap, space, base, cols, dtype):
        self.full, self.space, self.base, self.cols, self.dtype = full_ap, space, base, cols, dtype
        self.esz = ESZ[dtype]

    def v(self, c0=0, c1=None):
        c1 = self.cols if c1 is None else c1
        assert 0 <= c0 < c1 <= self.cols, (c0, c1, self.cols)
        return V(self.full[:, c0:c1], self.space, self.base + c0 * self.esz, self.base + c1 * self.esz)

    def v3(self, c0, n, stride, inner, i0=0):
        c0 += i0
        a0, r = divmod(c0, stride)
        assert self.cols % stride == 0 and r + inner <= stride and a0 + n <= self.cols // stride, (c0, n, stride, inner, self.cols)
        ap = self.full.rearrange("p (a b) -> p a b", b=stride)[:, a0:a0 + n, r:r + inner]
        c0 -= i0
        lo = self.base + (c0 + i0) * self.esz
        hi = self.base + (c0 + (n - 1) * stride + i0 + inner) * self.esz
        return V(ap, self.space, lo, hi)

    def parts(self, c0, n, stride, inner):
        return [self.v(c0 + a * stride, c0 + a * stride + inner) for a in range(n)]


class Rec:
    __slots__ = ("lo", "hi", "w", "sig", "dead")

    def __init__(self, lo, hi, w, sig):
        self.lo, self.hi, self.w, self.sig, self.dead = lo, hi, w, sig, False


class Op:
    __slots__ = ("fn", "waits", "sig", "inc")

    def __init__(self, fn, waits, sig, inc):
        self.fn, self.waits, self.sig, self.inc = fn, waits, sig, inc


class KB:
    def __init__(self, nc, sb_bytes, stack):
        self.nc = nc
        self.stack = stack
        self.ops = {e: [] for e in ENGS}
        self.pages = {"sb": {}, "ps": {}}
        self.waited = {e: {} for e in ENGS}
        self.sem_of = {}
        self.cnt = {}
        self.nsem = 0
        self.sb_words = sb_bytes // 4
        self.arena = stack.enter_context(nc.sbuf_tensor("arena", [128, self.sb_words], F32))
        self.psum = stack.enter_context(nc.psum_tensor("psum", [128, 8 * 512], F32))
        self.sb_top = 0
        self.nbank = 0
        self.defer = False
        self.deferred = []
        self.new_epoch()

    def alloc(self, cols, dtype):
        nbytes = (cols * ESZ[dtype] + 127) // 128 * 128
        base = self.sb_top
        self.sb_top += nbytes
        assert self.sb_top <= self.sb_words * 4, ("SBUF overflow", self.sb_top)
        return self.at(base, cols, dtype)

    def at(self, base, cols, dtype):
        assert base % 4 == 0
        w0 = base // 4
        nw = (cols * ESZ[dtype] + 3) // 4
        assert (w0 + nw) <= self.sb_words, ("SBUF overflow at", base, cols)
        ap = self.arena[:, w0:w0 + nw]
        if dtype != F32:
            ap = ap.bitcast(dtype)
        return Buf(ap, "sb", base, cols, dtype)

    def bank(self, i, dtype=F32):
        ap = self.psum[:, i * 512:(i + 1) * 512]
        cols = 512
        if dtype != F32:
            ap = ap.bitcast(dtype)
            cols = 512 * 4 // ESZ[dtype]
        return Buf(ap, "ps", i * 2048, cols, dtype)

    def nb(self, dtype=F32):
        b = self.bank(self.nbank % 8, dtype)
        self.nbank += 1
        return b

    def _newsem(self, name):
        s = self.stack.enter_context(self.nc.semaphore(f"{name}_{self.nsem}"))
        self.nsem += 1
        self.cnt[id(s)] = 0
        return s

    def new_epoch(self):
        for e in ("pe", "act", "dve", "pool"):
            self.sem_of[e] = self._newsem(e)

    def chan(self, name="dma"):
        return self._newsem(name)

    def _cands(self, v):
        pg = self.pages[v.space]
        seen = {}
        for p in range(v.lo // PAGE, (v.hi - 1) // PAGE + 1):
            lst = pg.get(p)
            if lst:
                for r in lst:
                    if not r.dead and r.lo < v.hi and v.lo < r.hi:
                        seen[id(r)] = r
        return seen.values()

    def _add(self, v, w, sig):
        r = Rec(v.lo, v.hi, w, sig)
        pg = self.pages[v.space]
        for p in range(v.lo // PAGE, (v.hi - 1) // PAGE + 1):
            lst = pg.get(p)
            if lst is None:
                pg[p] = [r]
            else:
                if len(lst) > 8:
                    lst[:] = [x for x in lst if not x.dead]
                lst.append(r)

    def flush(self, n=None):
        q = self.deferred
        k = len(q) if n is None else min(n, len(q))
        for _ in range(k):
            a = q.pop(0)
            self.emit(*a)

    def emit(self, eng, fn, reads=(), writes=(), chan=None):
        if self.defer:
            self.deferred.append((eng, fn, list(reads), list(writes), chan))
            return None
        if any(v.space == "ps" for v in reads) or any(v.space == "ps" for v in writes):
            nr, nw, seenb = [], [], set()
            for v in list(reads) + list(writes):
                if v.space == "ps":
                    for b in range(v.lo // 2048, (v.hi - 1) // 2048 + 1):
                        if b not in seenb:
                            seenb.add(b)
                            nw.append(V(None, "ps", b * 2048, b * 2048 + 2048))
            nr = [v for v in reads if v.space != "ps"]
            nw = nw + [v for v in writes if v.space != "ps"]
            reads, writes = nr, nw
        deps = {}
        for v in reads:
            for r in self._cands(v):
                if r.w:
                    k = id(r.sig[0])
                    if k not in deps or deps[k][1] < r.sig[1]:
                        deps[k] = r.sig
        for v in writes:
            for r in self._cands(v):
                k = id(r.sig[0])
                if k not in deps or deps[k][1] < r.sig[1]:
                    deps[k] = r.sig
        waits = []
        wd = self.waited[eng]
        own = self.sem_of.get(eng)
        for k, (sem, val) in deps.items():
            if eng == "pe" and sem is own:
                continue
            if wd.get(k, 0) >= val:
                continue
            wd[k] = val
            waits.append((sem, val))
        if chan is not None:
            prev = self.cnt[id(chan)]
            if prev > 0 and wd.get(id(chan), 0) < prev:
                wd[id(chan)] = prev
                waits.append((chan, prev))
            self.cnt[id(chan)] += 16
            sig = (chan, self.cnt[id(chan)])
            inc = 16
        else:
            sem = self.sem_of[eng]
            self.cnt[id(sem)] += 1
            sig = (sem, self.cnt[id(sem)])
            inc = 1
        self.ops[eng].append(Op(fn, waits, sig, inc))
        for v in writes:
            for r in self._cands(v):
                if v.lo <= r.lo and r.hi <= v.hi:
                    r.dead = True
            self._add(v, True, sig)
        for v in reads:
            for r in self._cands(v):
                if (not r.w) and r.sig[0] is sig[0] and v.lo <= r.lo and r.hi <= v.hi and chan is None:
                    r.dead = True
            self._add(v, False, sig)
        return sig

    def wait_sig(self, eng, sig):
        self.ops[eng].append(Op(None, [sig], None, 0))

    def finish(self):
        kb = self

        def run(engname):
            def body(eng):
                for op in kb.ops[engname]:
                    for (sem, val) in op.waits:
                        eng.wait_ge(sem, val)
                    if op.fn is None:
                        continue
                    ins = op.fn(eng)
                    ins.then_inc(op.sig[0], op.inc)
            return body

        with self.nc.Block() as block:
            block.tensor(run("pe"))
            block.scalar(run("act"))
            block.vector(run("dve"))
            block.gpsimd(run("pool"))
            block.sync(run("sp"))


def _aps(x):
    return x.ap if isinstance(x, V) else x


class Em:
    def __init__(self, kb):
        self.kb = kb

    def mm(self, out_v, terms):
        n = len(terms)

        def fn(e):
            ins = None
            for i, (l, r, o) in enumerate(terms):
                ins = e.matmul((o or out_v).ap, l.ap, r.ap, start=(i == 0), stop=(i == n - 1))
            return ins
        reads = [t[0] for t in terms] + [t[1] for t in terms]
        return self.kb.emit("pe", fn, reads=reads, writes=[out_v])

    def transpose(self, out_v, in_v, ident_v):
        return self.kb.emit("pe", lambda e: e.transpose(out=out_v.ap, in_=in_v.ap, identity=ident_v.ap),
                            reads=[in_v, ident_v], writes=[out_v])

    def act(self, out_v, in_v, func, scale=1.0, bias=None, rd=None, wr=None):
        reads = list(rd) if rd is not None else [in_v]
        kw = {}
        if isinstance(scale, V):
            reads.append(scale)
        if bias is not None:
            kw["bias"] = _aps(bias)
            if isinstance(bias, V):
                reads.append(bias)
        return self.kb.emit("act", lambda e: e.activation(out=out_v.ap, in_=in_v.ap, func=func, scale=_aps(scale), **kw),
                            reads=reads, writes=list(wr) if wr is not None else [out_v])

    def tt(self, eng, out_v, a, b, op, rd=None, wr=None):
        return self.kb.emit(eng, lambda e: e.tensor_tensor(out=out_v.ap, in0=a.ap, in1=b.ap, op=op),
                            reads=list(rd) if rd is not None else [a, b], writes=list(wr) if wr is not None else [out_v])

    def ts(self, eng, out_v, a, s1, op0, s2=None, op1=None, rd=None, wr=None):
        reads = list(rd) if rd is not None else [a]
        for s in (s1, s2):
            if isinstance(s, V):
                reads.append(s)
        kw = {}
        if op1 is not None:
            kw["op1"] = op1
        return self.kb.emit(eng, lambda e: e.tensor_scalar(out=out_v.ap, in0=a.ap, scalar1=_aps(s1), scalar2=_aps(s2), op0=op0, **kw),
                            reads=reads, writes=list(wr) if wr is not None else [out_v])

    def stt(self, out_v, a, s, b, op0, op1, rd=None, wr=None):
        reads = list(rd) if rd is not None else [a, b]
        if isinstance(s, V):
            reads.append(s)
        return self.kb.emit("dve", lambda e: e.scalar_tensor_tensor(out=out_v.ap, in0=a.ap, scalar=_aps(s), in1=b.ap, op0=op0, op1=op1),
                            reads=reads, writes=list(wr) if wr is not None else [out_v])

    def copy(self, eng, out_v, in_v, rd=None, wr=None):
        if eng == "act":
            f = lambda e: e.copy(out=out_v.ap, in_=in_v.ap)
        else:
            f = lambda e: e.tensor_copy(out=out_v.ap, in_=in_v.ap)
        return self.kb.emit(eng, f, reads=list(rd) if rd is not None else [in_v], writes=list(wr) if wr is not None else [out_v])

    def recip(self, out_v, in_v):
        return self.kb.emit("dve", lambda e: e.reciprocal(out=out_v.ap, in_=in_v.ap), reads=[in_v], writes=[out_v])

    def memset(self, eng, out_v, val):
        return self.kb.emit(eng, lambda e: e.memset(out_v.ap, val), writes=[out_v])

    def dma(self, eng, out, in_, chan, reads=(), writes=()):
        return self.kb.emit(eng, lambda e: e.dma_start(out=_aps(out), in_=_aps(in_)), reads=reads, writes=writes, chan=chan)


C_ID, C_SW, C_ONE = 0, 128, 256
C_SGN, C_NSGN = 384, 385
C_MASK = 386
C_RT = 394
C_E = 458
C_MLO, C_MHI = 522, 523
C_N = 524
VC_PB, VC_PS, VC_D, VC_GB, VC_L1G, VC_L1B, VC_L2G, VC_L2B, VC_N = 0, 4, 8, 12, 16, 24, 32, 40, 48


def make_consts():
    c = np.zeros((128, C_N), np.float32)
    c[:, C_ID:C_ID + 128] = np.eye(128, dtype=np.float32)
    for k in range(128):
        c[k, C_SW + (k + 64) % 128] = 1.0
    c[:, C_ONE:C_ONE + 128] = 1.0
    c[:64, C_SGN] = 1.0
    c[64:, C_SGN] = -1.0
    c[:, C_NSGN] = -c[:, C_SGN]
    for q in range(8):
        c[16 * q:16 * q + 16, C_MASK + q] = 1.0
    for wi, w in enumerate(WINS):
        for t in range(16):
            c[:, C_RT + wi * 16 + t] = np.float32(1.0) / np.float32(min(t + 1, w))
    for k in range(128):
        c[k, C_E + k % 64] = 1.0
    c[:64, C_MLO] = 1.0
    c[64:, C_MHI] = 1.0
    return c


def build(nseq, nlayer):
    nc = bass.Bass("TRN2", target_bir_lowering=False)

    def din(name, shape):
        return nc.dram_tensor(name, shape, F32, kind="ExternalInput").ap()

    x_d = din("x", [nseq, SEQ, D])
    p_d = din("p", [nlayer, nseq, SEQ, 256])
    consts_d = din("consts", [128, C_N])
    vecs_d = din("vecs", [nlayer, 128, VC_N])
    ssa_d = din("ssm_a", [nlayer, 128, 96])
    ssb_d = din("ssm_b", [nlayer, 128, 1024])
    ssc_d = din("ssm_c", [nlayer, 128, 512])
    win_d = din("w_in_t", [nlayer, 8, 128, 1024])
    wout_d = din("w_out_t", [nlayer, 8, 128, 1024])
    gate_d = din("gate_t", [nlayer, 8, 128, 1024])
    w1_d = din("w1_t", [nlayer, 22, 128, 1024])
    w3_d = din("w3_t", [nlayer, 22, 128, 1024])
    w2_d = din("w2_t", [nlayer, 3, 8, 128, 1024])
    ple_d = din("ple_t", [nlayer, 8, 128, 256])
    glu_d = din("glu_t", [nlayer, 128, 2048])
    poolw_d = din("pool_t", [nlayer, 128, 512])
    out_d = nc.dram_tensor("out", [nseq, SEQ, D], F32, kind="ExternalOutput").ap()

    with contextlib.ExitStack() as st:
        kb = KB(nc, 206 * 1024, st)
        em = Em(kb)
        cst = kb.alloc(C_N, F32)
        id16 = kb.alloc(128, BF16)
        one16 = kb.alloc(128, BF16)
        vecs = kb.alloc(VC_N, F32)
        hT32 = kb.alloc(8 * SEQ, F32)
        hT16 = kb.alloc(8 * SEQ, BF16)
        mix16 = kb.alloc(8 * SEQ, BF16)
        NS = 8
        ring = [kb.alloc(1024, BF16) for _ in range(NS)]
        glu16 = kb.alloc(2048, BF16)
        poolw16 = kb.alloc(512, BF16)
        ccpad = kb.alloc(8 * 128, BF16)
        S0 = kb.sb_top
        K = 1024

        def sc(off, cols, dtype):
            return kb.at(S0 + off, cols, dtype)
        u32 = sc(0, SEQ, F32)
        sA = sc(8 * K, SEQ, F32)
        sB = sc(16 * K, SEQ, F32)
        z16 = [sc(24 * K, TW, BF16), sc(25 * K, TW, BF16)]
        XD = [sc(5 * K * i, SEQ, BF16) for i in range(4)]
        HB = [sc(5 * K * i + 4 * K, TW, BF16) for i in range(4)]
        mats = [sc(20 * K + 3 * K * i, 12 * 128, BF16) for i in range(2)]
        bbtpad = sc(50 * K, 8 * 128, BF16)
        ussm32 = sc(26 * K, SEQ, F32)
        ussm16 = sc(34 * K, SEQ, BF16)
        bx = sc(26 * K, 1024, F32)
        cc32 = sc(30 * K, 512, F32)
        bb32 = sc(32 * K, 512, F32)
        tmp32 = sc(34 * K, 32 * 40, F32)
        pwr = sc(42 * K, 16 * 32, F32)
        pwi = sc(44 * K, 16 * 32, F32)
        bbt = sc(46 * K, 4 * 128, BF16)
        cc16 = sc(47 * K, 512, BF16)
        bb16 = sc(48 * K, 512, BF16)
        ssa = sc(49 * K, 96, F32)
        lx16 = sc(0, 8 * TW, BF16)
        lsq16 = sc(8 * K, 8 * TW, BF16)
        lmean = sc(16 * K, TW, F32)
        lt1 = sc(18 * K, TW, F32)
        lrstd = sc(20 * K, TW, F32)
        pstage = sc(0, 16 * 256, F32)
        pT16 = sc(16 * K, 2 * SEQ, BF16)
        fA = [sc(24 * K + 2 * K * i, TW, F32) for i in range(4)]
        xstage = [sc(0, D, F32), sc(4 * K, D, F32)]
        assert S0 + 50 * K <= 206 * K, S0

        ident32 = cst.v(C_ID, C_ID + 128)
        swap32 = cst.v(C_SW, C_SW + 128)

        ch_c = kb.chan("c")
        ch_ring = [kb.chan("r") for _ in range(NS)]
        ch_misc = [kb.chan("m") for _ in range(6)]
        ch_xin = [kb.chan("xi") for _ in range(2)]
        ch_out = [kb.chan("xo") for _ in range(2)]
        ch_p = kb.chan("p")

        em.dma("sp", cst.v(), consts_d, ch_c, writes=[cst.v()])
        em.copy("dve", id16.v(), ident32)
        em.copy("dve", one16.v(), cst.v(C_ONE, C_ONE + 128))
        if "m" not in KSKIP:
            em.memset("pool", ccpad.v(), 0.0)

        chunks = []
        for s in range(nseq):
            for l in range(nlayer):
                for m in range(8):
                    chunks.append((win_d[l, m], 8))
                for m in range(8):
                    chunks.append((wout_d[l, m], 8))
                for pi, (f0, fn_) in enumerate(FPASS):
                    for f in range(f0, f0 + fn_):
                        chunks.append((w1_d[l, f], 8))
                        chunks.append((w3_d[l, f], 8))
                    for m in range(8):
                        chunks.append((w2_d[l, pi, m], fn_))
                for m in range(8):
                    chunks.append((gate_d[l, m], 8))
                    chunks.append((ple_d[l, m], 2))
        wstate = {"issued": 0, "next": 0}

        def w_issue(upto):
            while wstate["issued"] < min(upto, len(chunks)):
                i = wstate["issued"]
                src, kt = chunks[i]
                slot = ring[i % NS]
                dst = slot.v(0, kt * 128)
                em.dma("pool", dst, src[:, 0:kt * 128], ch_ring[i % NS], writes=[dst])
                wstate["issued"] += 1

        def w_next(kt):
            i = wstate["next"]
            assert chunks[i][1] == kt, (i, chunks[i][1], kt)
            w_issue(i + NS - 1)
            wstate["next"] += 1
            return ring[i % NS]

        if PH >= 0:
            w_issue(NS - 2)

        def load_x(s):
            for tt in range(16):
                stg = xstage[tt % 2]
                em.dma("sp", stg.v(), x_d[s, tt * 128:(tt + 1) * 128, :], ch_xin[tt % 2], writes=[stg.v()])
                for half in range(2):
                    bk = kb.nb()
                    for j in range(4):
                        m = half * 4 + j
                        em.transpose(bk.v(j * 128, j * 128 + 128), stg.v(m * 128, m * 128 + 128), ident32)
                    o32 = hT32.v3((half * 4) * SEQ + tt * 128, 4, SEQ, 128)
                    o16 = hT16.v3((half * 4) * SEQ + tt * 128, 4, SEQ, 128)
                    src = bk.v3(0, 4, 128, 128)
                    wr32 = hT32.parts((half * 4) * SEQ + tt * 128, 4, SEQ, 128)
                    wr16 = hT16.parts((half * 4) * SEQ + tt * 128, 4, SEQ, 128)
                    em.copy("act", o32, src, rd=[bk.v()], wr=wr32)
                    em.copy("dve", o16, src, rd=[bk.v()], wr=wr16)

        def store_out(s):
            for tt in range(16):
                stg = xstage[tt % 2]
                for half in range(2):
                    bk = kb.nb()
                    for j in range(4):
                        m = half * 4 + j
                        em.transpose(bk.v(j * 128, j * 128 + 128), hT32.v(m * SEQ + tt * 128, m * SEQ + tt * 128 + 128), ident32)
                    eng = "act" if half == 0 else "dve"
                    em.copy(eng, stg.v(half * 512, half * 512 + 512), bk.v())
                sig = em.dma("sp", out_d[s, tt * 128:(tt + 1) * 128, :], stg.v(), ch_out[tt % 2], reads=[stg.v()])
                final_sigs[tt % 2] = sig

        final_sigs = [None, None]

        def ssm_prep(l, stg):
            bx = sc(stg, 1024, F32)
            cc32 = sc(stg + 4 * K, 512, F32)
            bb32 = sc(stg + 6 * K, 512, F32)
            tmp32 = sc(stg + 8 * K, 32 * 40, F32)
            em.dma("sp", ssa.v(), ssa_d[l], ch_misc[0], writes=[ssa.v()])
            em.dma("sp", bx.v(), ssb_d[l], ch_misc[1], writes=[bx.v()])
            em.dma("sp", cc32.v(), ssc_d[l], ch_misc[2], writes=[cc32.v()])
            T = [tmp32.v(32 * i, 32 * i + 32) for i in range(40)]
            are, aim, ldt = ssa.v(0, 32), ssa.v(32, 64), ssa.v(64, 96)
            arec, dt, tre, er, ang = T[0], T[1], T[2], T[3], T[4]
            em.ts("dve", arec, are, -1e-4, ALU.min)
            em.act(dt, ldt, AF.Exp)
            em.tt("dve", tre, arec, dt, ALU.mult)
            em.act(er, tre, AF.Exp)
            em.tt("dve", ang, aim, dt, ALU.mult)
            ki = kb.at(tmp32.base + 32 * 4 * 39, 32, I32)

            def sin_of(dst, src, shift, t0, t1):
                em.ts("dve", t0, src, shift, ALU.add)
                em.ts("dve", ki.v(), t0, 1.0 / (2 * math.pi), ALU.mult)
                em.copy("dve", t1, ki.v())
                em.stt(t1, t1, -2.0 * math.pi, t0, ALU.mult, ALU.add)
                em.ts("dve", t0, t1, math.pi, ALU.is_gt, -2.0 * math.pi, ALU.mult)
                em.tt("dve", t1, t1, t0, ALU.add)
                em.ts("dve", t0, t1, -math.pi, ALU.is_lt, 2.0 * math.pi, ALU.mult)
                em.tt("dve", t1, t1, t0, ALU.add)
                em.ts("dve", t1, t1, 3.1415925, ALU.min, -3.1415925, ALU.max)
                em.act(dst, t1, AF.Sin)
            sinv, cosv = T[5], T[6]
            sin_of(sinv, ang, 0.0, T[7], T[8])
            sin_of(cosv, ang, math.pi / 2, T[7], T[8])
            lr, li = T[9], T[10]
            em.tt("dve", lr, er, cosv, ALU.mult)
            em.tt("dve", li, er, sinv, ALU.mult)
            xm1, den, rden, a1, a2, cr, ci = T[11], T[12], T[13], T[14], T[15], T[16], T[17]
            em.ts("dve", xm1, lr, -1.0, ALU.add)
            em.tt("dve", den, arec, arec, ALU.mult)
            em.tt("dve", a1, aim, aim, ALU.mult)
            em.tt("dve", den, den, a1, ALU.add)
            em.recip(rden, den)
            em.tt("dve", a1, xm1, arec, ALU.mult)
            em.tt("dve", a2, li, aim, ALU.mult)
            em.tt("dve", a1, a1, a2, ALU.add)
            em.tt("dve", cr, a1, rden, ALU.mult)
            em.tt("dve", a1, li, arec, ALU.mult)
            em.tt("dve", a2, xm1, aim, ALU.mult)
            em.tt("dve", a1, a1, a2, ALU.subtract)
            em.tt("dve", ci, a1, rden, ALU.mult)
            c1, c2 = T[18], T[19]
            em.ts("dve", c1, cr, cst.v(C_SGN, C_SGN + 1), ALU.mult)
            em.ts("dve", c2, ci, -1.0, ALU.mult)
            c1b = V(c1.ap.unsqueeze(2).broadcast_to([128, 32, 16]), "sb", c1.lo, c1.hi)
            c2b = V(c2.ap.unsqueeze(2).broadcast_to([128, 32, 16]), "sb", c2.lo, c2.hi)
            bx1 = bx.v3(0, 32, 16, 16)
            bx2 = bx.v3(512, 32, 16, 16)
            em.tt("dve", bb32.v3(0, 32, 16, 16), bx1, c1b, ALU.mult)
            em.tt("dve", bx2, bx2, c2b, ALU.mult)
            em.tt("dve", bb16.v(), bb32.v(), bx.v(512, 1024), ALU.add)
            em.copy("dve", cc16.v(), cc32.v())
            bk = kb.nb(BF16)
            for t in range(4):
                em.transpose(bk.v(t * 128, t * 128 + 128), bb16.v(t * 128, t * 128 + 128), id16.v())
            em.copy("dve", bbt.v(), bk.v(0, 512))
            pw = {1: (lr, li)}
            nxt = [20]

            def newt():
                i = nxt[0]
                nxt[0] += 1
                return T[i]

            def csq(a):
                re, im = newt(), newt()
                em.tt("dve", T[38], a[1], a[1], ALU.mult)
                em.tt("dve", re, a[0], a[0], ALU.mult)
                em.tt("dve", re, re, T[38], ALU.subtract)
                em.stt(im, a[0], 2.0, a[1], ALU.mult, ALU.mult)
                return (re, im)

            def cmul(a, b, re, im):
                em.tt("dve", T[38], a[1], b[1], ALU.mult)
                em.tt("dve", re, a[0], b[0], ALU.mult)
                em.tt("dve", re, re, T[38], ALU.subtract)
                em.tt("dve", T[38], a[1], b[0], ALU.mult)
                em.tt("dve", im, a[0], b[1], ALU.mult)
                em.tt("dve", im, im, T[38], ALU.add)

            mlo, mhi = cst.v(C_MLO, C_MLO + 1), cst.v(C_MHI, C_MHI + 1)

            def store(p, a):
                i = PWIDX[p]
                c0, c1 = pwr.v(32 * i, 32 * i + 32), pwi.v(32 * i, 32 * i + 32)
                em.ts("dve", T[36], a[1], cst.v(C_NSGN, C_NSGN + 1), ALU.mult)
                em.ts("dve", T[37], T[36], mhi, ALU.mult)
                em.stt(c0, a[0], mlo, T[37], ALU.mult, ALU.add)
                em.ts("dve", T[37], a[0], mhi, ALU.mult)
                em.stt(c1, T[36], mlo, T[37], ALU.mult, ALU.add)
            cur = pw[1]
            store(1, cur)
            e = 1
            while e < 1024:
                nxt[0] = 20 + (int(math.log2(e)) % 2) * 6
                sq = csq(cur)
                store(2 * e, sq) if (2 * e) in PWIDX else None
                if (3 * e) in PWIDX:
                    t3 = (newt(), newt())
                    cmul(sq, cur, t3[0], t3[1])
                    store(3 * e, t3)
                cur = sq
                e *= 2

        def gen_mats(stage_i, g0, buf):
            s, r = STAGES[stage_i]
            e32 = cst.v(C_E, C_E + 64)
            eb = V(e32.ap.unsqueeze(1).broadcast_to([128, 4, 64]), "sb", e32.lo, e32.hi)
            for k in range(1, r):
                pi = PWIDX[s * k]
                for h, src in ((0, pwr), (1, pwi)):
                    a = src.v(32 * pi + g0, 32 * pi + g0 + 4)
                    ab = V(a.ap.unsqueeze(2).broadcast_to([128, 4, 64]), "sb", a.lo, a.hi)
                    c0 = (k - 1) * 128 + h * 64
                    o = buf.v3(c0, 4, 384, 64)
                    em.tt("pool", o, eb, ab, ALU.mult, wr=buf.parts(c0, 4, 384, 64))

        gm_t = [sc(38 * K, 512, F32), sc(40 * K, 512, F32)]

        def ssm_tile(l, mt, evq):
            for q in range(8):
                em.ts("pool", bbtpad.v(q * 128, q * 128 + 128), bbt.v(mt * 128, mt * 128 + 128), cst.v(C_MASK + q, C_MASK + q + 1), ALU.mult)
            for q in range(8):
                g = mt * 8 + q
                em.copy("pool", ccpad.v(q * 128 + q * 16, q * 128 + q * 16 + 16), cc16.v(g * 16, g * 16 + 16))
            dcol = vecs.v(VC_D + mt, VC_D + mt + 1)
            def evac(dst, bk):
                ev = evq[0] % 2
                evq[0] += 1
                em.copy("act" if ev == 0 else "dve", dst, bk.v())

            def evac_add(dst, bk, addsrc, c0=0):
                em.tt("dve", dst, bk.v(c0, TW), addsrc, ALU.add)

            for b in range(2):
                g0 = mt * 8 + b * 4
                gen_mats(0, g0, mats[0])

                def M(buf, gi, k):
                    return buf.v(gi * 384 + (k - 1) * 128, gi * 384 + k * 128)
                for gi in range(4):
                    q = b * 4 + gi
                    for r in range(4):
                        bk = kb.nb()
                        em.mm(bk.v(), [(bbtpad.v(q * 128, q * 128 + 128), ussm16.v(r * TW, r * TW + TW), None)])
                        em.copy("act", XD[gi].v(r * TW, r * TW + TW), bk.v())
                gen_mats(1, g0, mats[1])
                for gi in range(4):
                    bk = kb.nb()
                    fold = (gi % 2 == 0)
                    terms = [] if fold else [(id16.v(), XD[gi].v(3 * TW, 4 * TW), None)]
                    for k in range(1, 4):
                        terms.append((M(mats[0], gi, k), XD[gi].v((3 - k) * TW, (4 - k) * TW), None))
                    em.mm(bk.v(), terms)
                    if fold:
                        evac_add(HB[gi].v(), bk, XD[gi].v(3 * TW, 4 * TW))
                    else:
                        em.copy("act", HB[gi].v(), bk.v())
                for si in range(1, len(STAGES)):
                    s_, r_ = STAGES[si]
                    mb = mats[si % 2]
                    gen_mats(si + 1 if si + 1 < len(STAGES) else 0, g0, mats[(si + 1) % 2])
                    for gi in range(4):
                        bk = kb.nb()
                        sh1 = s_ // 4
                        fold = (gi % 2 == 0)
                        terms = [] if fold else [(id16.v(), HB[gi].v(), None)]
                        for k in range(1, r_):
                            sh = s_ * k // 4
                            terms.append((M(mb, gi, k), HB[gi].v(0, TW - sh), bk.v(sh, TW)))
                        if fold:
                            em.mm(bk.v(sh1, TW), terms)
                            evac_add(HB[gi].v(sh1, TW), bk, HB[gi].v(sh1, TW), sh1)
                        else:
                            em.mm(bk.v(), terms)
                            em.copy("act", HB[gi].v(), bk.v())
                m0 = mats[len(STAGES) % 2]
                for gi in range(4):
                    for r in (2, 1, 0):
                        bk = kb.nb()
                        fold = (gi % 2 == 0)
                        terms = [] if fold else [(id16.v(), XD[gi].v(r * TW, (r + 1) * TW), None)]
                        for k in range(1, r + 1):
                            terms.append((M(m0, gi, k), XD[gi].v((r - k) * TW, (r - k + 1) * TW), None))
                        terms.append((M(m0, gi, r + 1), HB[gi].v(0, TW - 1), bk.v(1, TW)))
                        if not fold:
                            em.mm(bk.v(), terms)
                            em.copy("act", XD[gi].v((r + 1) * TW, (r + 2) * TW), bk.v())
                        elif r == 0:
                            em.mm(bk.v(1, TW), terms)
                            evac_add(XD[gi].v(TW + 1, 2 * TW), bk, XD[gi].v(1, TW), 1)
                            em.copy("act", XD[gi].v(TW, TW + 1), XD[gi].v(0, 1))
                        else:
                            em.mm(bk.v(), terms)
                            evac_add(XD[gi].v((r + 1) * TW, (r + 2) * TW), bk, XD[gi].v(r * TW, (r + 1) * TW))
                for r in range(4):
                    bk = kb.nb()
                    terms = []
                    for gi in range(4):
                        src = HB[gi].v() if r == 3 else XD[gi].v((r + 1) * TW, (r + 2) * TW)
                        terms.append((ccpad.v((b * 4 + gi) * 128, (b * 4 + gi) * 128 + 128), src, None))
                    em.mm(bk.v(), terms)
                    u_ap = ussm32.full.rearrange("p (j r) -> p r j", r=4)[:, r, :]
                    uv = V(u_ap, "sb", ussm32.base, ussm32.base + 4 * SEQ)
                    if b == 0:
                        em.stt(uv, uv, dcol, bk.v(), ALU.mult, ALU.add)
                    else:
                        em.tt("dve", uv, uv, bk.v(), ALU.add)
            for n in range(NT):
                em.act(mix16.v((4 + mt) * SEQ + n * TW, (4 + mt) * SEQ + n * TW + TW), ussm32.v(n * TW, n * TW + TW), AF.Gelu_apprx_tanh)

        def pool_tile(l, g):
            w = WINS[g]
            src = u32
            bufs = [sA, sB]
            bi = 0
            step = 1
            while step < w:
                dst = bufs[bi]
                em.tt("dve", dst.v(step, SEQ), src.v(step, SEQ), src.v(0, SEQ - step), ALU.add)
                em.copy("pool", dst.v(0, step), src.v(0, step))
                src = dst
                bi ^= 1
                step *= 2
            zb = bufs[bi]
            em.stt(zb.v(), src.v(), 1.0 / w, u32.v(), ALU.mult, ALU.subtract)
            em.tt("dve", zb.v(0, 16), src.v(0, 16), cst.v(C_RT + g * 16, C_RT + g * 16 + 16), ALU.mult)
            em.tt("dve", zb.v(0, 16), zb.v(0, 16), u32.v(0, 16), ALU.subtract)
            for n in range(NT):
                zz = z16[n % 2]
                em.copy("act", zz.v(), zb.v(n * TW, n * TW + TW))
                bk = kb.nb()
                em.mm(bk.v(), [(poolw16.v(g * 128, g * 128 + 128), zz.v(), None)])
                em.ts("dve", mix16.v(g * SEQ + n * TW, g * SEQ + n * TW + TW), bk.v(),
                      vecs.v(VC_PB + g, VC_PB + g + 1), ALU.add, vecs.v(VC_PS + g, VC_PS + g + 1), ALU.mult)

        LX = [sc(0, 8 * TW, BF16), sc(16 * K, 8 * TW, BF16)]
        LSQ = [sc(8 * K, 8 * TW, BF16), sc(24 * K, 8 * TW, BF16)]
        LMEAN = [sc(32 * K, TW, F32), sc(34 * K, TW, F32)]
        LRSTD = [sc(36 * K, TW, F32), sc(38 * K, TW, F32)]
        LT1 = sc(40 * K, TW, F32)

        def layer_norm(gcol, bcol):
            def A(n):
                c0 = n * TW
                xin = hT32.v3(c0, 8, SEQ, TW)
                xparts = hT32.parts(c0, 8, SEQ, TW)
                lx16, lsq16, lmean, lrstd = LX[n % 2], LSQ[n % 2], LMEAN[n % 2], LRSTD[n % 2]
                em.act(lsq16.v3(0, 8, TW, TW), xin, AF.Square, rd=xparts, wr=[lsq16.v()])
                em.copy("pool", lx16.v3(0, 8, TW, TW), xin, rd=xparts, wr=[lx16.v()])
                bs = kb.nb()
                bq = kb.nb()
                em.mm(bs.v(), [(one16.v(), lx16.v(m * TW, m * TW + TW), None) for m in range(8)])
                em.mm(bq.v(), [(one16.v(), lsq16.v(m * TW, m * TW + TW), None) for m in range(8)])
                em.ts("dve", lmean.v(), bs.v(), 1.0 / D, ALU.mult)
                em.tt("dve", LT1.v(), lmean.v(), lmean.v(), ALU.mult)
                em.stt(LT1.v(), bq.v(), 1.0 / D, LT1.v(), ALU.mult, ALU.subtract)
                em.act(LT1.v(), LT1.v(), AF.Sqrt, bias=LN_EPS)
                em.recip(lrstd.v(), LT1.v())

            def B(n):
                c0 = n * TW
                xin = hT32.v3(c0, 8, SEQ, TW)
                xparts = hT32.parts(c0, 8, SEQ, TW)
                lmean, lrstd = LMEAN[n % 2], LRSTD[n % 2]
                mb_ = V(lmean.v().ap.unsqueeze(1).broadcast_to([128, 8, TW]), "sb", lmean.base, lmean.base + 2048)
                rb_ = V(lrstd.v().ap.unsqueeze(1).broadcast_to([128, 8, TW]), "sb", lrstd.base, lrstd.base + 2048)
                em.tt("pool", xin, xin, mb_, ALU.subtract, rd=xparts + [lmean.v()], wr=xparts)
                em.tt("dve", xin, xin, rb_, ALU.mult, rd=xparts + [lrstd.v()], wr=xparts)
                for m in range(8):
                    hv = hT32.v(m * SEQ + c0, m * SEQ + c0 + TW)
                    em.act(hv, hv, AF.Identity, scale=vecs.v(gcol + m, gcol + m + 1), bias=vecs.v(bcol + m, bcol + m + 1))
                em.copy("act", hT16.v3(c0, 8, SEQ, TW), xin, rd=xparts, wr=hT16.parts(c0, 8, SEQ, TW))
            A(0)
            A(1)
            B(0)
            A(2)
            B(1)
            A(3)
            B(2)
            B(3)

        def layer(s, l, first, next_prep):
            kb.new_epoch()
            em.dma("sp", vecs.v(), vecs_d[l], ch_misc[3], writes=[vecs.v()])
            em.dma("pool", glu16.v(), glu_d[l], ch_misc[4], writes=[glu16.v()])
            em.dma("pool", poolw16.v(), poolw_d[l], ch_misc[5], writes=[poolw16.v()])
            if PH < 1:
                return
            if first:
                ssm_prep(l, 26 * K)
            evq = [0]
            if PH < 2:
                return
            for m in range(8):
                wc = w_next(8)
                for n in range(NT):
                    bk = kb.nb()
                    em.mm(bk.v(), [(wc.v(k * 128, k * 128 + 128), hT16.v(k * SEQ + n * TW, k * SEQ + n * TW + TW), None) for k in range(8)])
                    if m < 4:
                        em.copy("act", u32.v(n * TW, n * TW + TW), bk.v())
                    else:
                        em.copy("act", ussm32.v(n * TW, n * TW + TW), bk.v())
                        o_ap = ussm16.full.rearrange("p (r j) -> p r j", r=4)[:, :, n * 128:(n + 1) * 128]
                        i_ap = bk.full.rearrange("p (j r) -> p r j", r=4)
                        em.copy("dve", V(o_ap, "sb", ussm16.base, ussm16.base + 4096), V(i_ap, "ps", bk.base, bk.base + 2048))
                if m < 4:
                    pool_tile(l, m)
                elif PH >= 3:
                    ssm_tile(l, m - 4, evq)
            if PH < 4:
                return
            for n in range(NT):
                bks = []
                for m in range(4):
                    bk = kb.nb()
                    em.mm(bk.v(), [(glu16.v(k * 512 + m * 128, k * 512 + m * 128 + 128),
                                    mix16.v((4 + k) * SEQ + n * TW, (4 + k) * SEQ + n * TW + TW), None) for k in range(4)])
                    bks.append(bk)
                for m in range(4):
                    sg = fA[m % 4]
                    em.act(sg.v(), bks[m].v(), AF.Sigmoid, bias=vecs.v(VC_GB + m, VC_GB + m + 1))
                    mv = mix16.v((4 + m) * SEQ + n * TW, (4 + m) * SEQ + n * TW + TW)
                    em.tt("dve", mv, mv, sg.v(), ALU.mult)
            for m in range(8):
                wc = w_next(8)
                for n in range(NT):
                    bk = kb.nb()
                    em.mm(bk.v(), [(wc.v(k * 128, k * 128 + 128), mix16.v(k * SEQ + n * TW, k * SEQ + n * TW + TW), None) for k in range(8)])
                    hv = hT32.v(m * SEQ + n * TW, m * SEQ + n * TW + TW)
                    em.stt(hv, hv, ALPHA, bk.v(), ALU.mult, ALU.add)
            if PH < 5:
                return
            layer_norm(VC_L1G, VC_L1B)
            if PH < 6:
                return
            em.dma("sp", pstage.v3(0, 16, 256, 256), p_d[l, s].rearrange("(t q) c -> q t c", q=128), ch_p, writes=[pstage.v()])
            for kt in range(2):
                for quarter in range(4):
                    bk = kb.nb()
                    for j in range(4):
                        tt = quarter * 4 + j
                        em.transpose(bk.v(j * 128, j * 128 + 128), pstage.v(tt * 256 + kt * 128, tt * 256 + kt * 128 + 128), ident32)
                    em.copy("act", pT16.v(kt * SEQ + quarter * 512, kt * SEQ + quarter * 512 + 512), bk.v())
            if PH < 7:
                return
            per_it = [0]
            for pi, (f0, fn_) in enumerate(FPASS):
                for fl in range(fn_):
                    w1c = w_next(8)
                    w3c = w_next(8)
                    for n in range(NT):
                        ba = kb.nb()
                        bb_ = kb.nb()
                        em.mm(ba.v(), [(w1c.v(k * 128, k * 128 + 128), hT16.v(k * SEQ + n * TW, k * SEQ + n * TW + TW), None) for k in range(8)])
                        em.mm(bb_.v(), [(w3c.v(k * 128, k * 128 + 128), hT16.v(k * SEQ + n * TW, k * SEQ + n * TW + TW), None) for k in range(8)])
                        sa_ = fA[(fl * NT + n) % 4]
                        em.act(sa_.v(), ba.v(), AF.Silu)
                        em.tt("dve", mix16.v(fl * SEQ + n * TW, fl * SEQ + n * TW + TW), sa_.v(), bb_.v(), ALU.mult)
                        kb.flush(per_it[0])
                for m in range(8):
                    w2c = w_next(fn_)
                    for n in range(NT):
                        bk = kb.nb()
                        em.mm(bk.v(), [(w2c.v(k * 128, k * 128 + 128), mix16.v(k * SEQ + n * TW, k * SEQ + n * TW + TW), None) for k in range(fn_)])
                        hv = hT32.v(m * SEQ + n * TW, m * SEQ + n * TW + TW)
                        if pi == 0:
                            em.stt(hv, hv, ALPHA, bk.v(), ALU.mult, ALU.add)
                        else:
                            em.tt("dve", hv, hv, bk.v(), ALU.add)
                        if pi == len(FPASS) - 1:
                            em.copy("act", hT16.v(m * SEQ + n * TW, m * SEQ + n * TW + TW), hv)
                        kb.flush(per_it[0])
                if pi == 0 and next_prep is not None:
                    kb.defer = True
                    ssm_prep(next_prep, 0)
                    kb.defer = False
                    per_it[0] = len(kb.deferred) // 100 + 1
            kb.flush()
            for m in range(8):
                wg = w_next(8)
                wp = w_next(2)
                for n in range(NT):
                    bg = kb.nb()
                    be = kb.nb()
                    em.mm(bg.v(), [(wg.v(k * 128, k * 128 + 128), hT16.v(k * SEQ + n * TW, k * SEQ + n * TW + TW), None) for k in range(8)])
                    em.mm(be.v(), [(wp.v(k * 128, k * 128 + 128), pT16.v(k * SEQ + n * TW, k * SEQ + n * TW + TW), None) for k in range(2)])
                    sg = fA[(m * NT + n) % 4]
                    em.act(sg.v(), bg.v(), AF.Sigmoid)
                    em.tt("dve", sg.v(), sg.v(), be.v(), ALU.mult)
                    hv = hT32.v(m * SEQ + n * TW, m * SEQ + n * TW + TW)
                    em.tt("dve", hv, hv, sg.v(), ALU.add)
            layer_norm(VC_L2G, VC_L2B)

        for s in range(nseq):
            if "x" in KSKIP:
                continue
            load_x(s)
            for l in range(nlayer):
                if PH >= 0:
                    idx = s * nlayer + l
                    layer(s, l, idx == 0, ((idx + 1) % nlayer) if idx + 1 < nseq * nlayer else None)
            store_out(s)
        for sig in final_sigs:
            if sig is not None:
                kb.wait_sig("sp", sig)
        kb.finish()
    return nc


def _tile_w(W):
    Kd, Md = W.shape
    return np.ascontiguousarray(W.reshape(Kd // 128, 128, Md // 128, 128).transpose(2, 1, 0, 3)).reshape(Md // 128, 128, Kd)


def prep_weights(inp, layers):
    f = lambda a: np.asarray(a, np.float32)
    L = len(layers)
    o = {}
    o["consts"] = make_consts()
    vec = np.zeros((L, 128, VC_N), np.float32)
    ssa = np.zeros((L, 128, 96), np.float32)
    ssb = np.zeros((L, 128, 1024), np.float32)
    ssc = np.zeros((L, 128, 512), np.float32)
    for i, l in enumerate(layers):
        def colize(v, ntile):
            return f(v).reshape(ntile, 128).T
        vec[i, :, VC_PB:VC_PB + 4] = colize(inp["pool_b"][l], 4)
        vec[i, :, VC_PS:VC_PS + 4] = colize(inp["pool_scale"][l], 4)
        vec[i, :, VC_D:VC_D + 4] = colize(inp["ssm_d"][l], 4)
        vec[i, :, VC_GB:VC_GB + 4] = colize(inp["ssm_glu_b"][l], 4)
        vec[i, :, VC_L1G:VC_L1G + 8] = colize(inp["ln1_g"][l], 8)
        vec[i, :, VC_L1B:VC_L1B + 8] = colize(inp["ln1_b"][l], 8)
        vec[i, :, VC_L2G:VC_L2G + 8] = colize(inp["ln2_g"][l], 8)
        vec[i, :, VC_L2B:VC_L2B + 8] = colize(inp["ln2_b"][l], 8)
        are = f(inp["ssm_a_re"][l]).T
        aim = f(inp["ssm_a_im"][l]).T
        ssa[i, :, 0:32] = np.concatenate([are, are], 0)
        ssa[i, :, 32:64] = np.concatenate([aim, aim], 0)
        ssa[i, :, 64:96] = np.broadcast_to(f(inp["ssm_log_dt"][l])[None, :], (128, 32))
        bre = f(inp["ssm_b_re"][l]).transpose(1, 0, 2).reshape(64, 512)
        bim = f(inp["ssm_b_im"][l]).transpose(1, 0, 2).reshape(64, 512)
        ssb[i, :, 0:512] = np.concatenate([bre, bim], 0)
        ssb[i, :, 512:1024] = np.concatenate([bim, bre], 0)
        cre = f(inp["ssm_c_re"][l]).transpose(2, 0, 1).reshape(64, 512)
        cim = f(inp["ssm_c_im"][l]).transpose(2, 0, 1).reshape(64, 512)
        ssc[i] = np.concatenate([cre, cim], 0)
    o["vecs"], o["ssm_a"], o["ssm_b"], o["ssm_c"] = vec, ssa, ssb, ssc
    o["w_in_t"] = np.stack([_tile_w(f(inp["w_in"][l])) for l in layers])
    o["w_out_t"] = np.stack([_tile_w(f(inp["w_out"][l])) for l in layers])
    o["gate_t"] = np.stack([_tile_w(f(inp["ple_gate_w"][l])) for l in layers])
    o["w1_t"] = np.stack([_tile_w(f(inp["ffn_w1"][l])) for l in layers])
    o["w3_t"] = np.stack([_tile_w(f(inp["ffn_w3"][l])) for l in layers])
    w2 = np.zeros((L, 3, 8, 128, 1024), np.float32)
    for i, l in enumerate(layers):
        W2 = f(inp["ffn_w2"][l])
        for pi, (f0, fn_) in enumerate(FPASS):
            w2[i, pi, :, :, :fn_ * 128] = _tile_w(W2[f0 * 128:(f0 + fn_) * 128, :])
    o["w2_t"] = w2
    o["ple_t"] = np.stack([_tile_w(f(inp["ple_w"][l])) for l in layers])
    o["glu_t"] = np.stack([np.ascontiguousarray(f(inp["ssm_glu_w"][l]).reshape(4, 128, 512).transpose(1, 0, 2)).reshape(128, 2048) for l in layers])
    o["pool_t"] = np.stack([np.ascontiguousarray(f(inp["pool_w"][l]).transpose(1, 0, 2)).reshape(128, 512) for l in layers])
    return o


_NC_CACHE = {}


def kernel(**inputs):
    x = np.asarray(inputs["x"], np.float32)
    p = np.asarray(inputs["p"], np.float32)
    B = x.shape[0]
    nseq = B // NCORES
    w = prep_weights(inputs, list(range(DEPTH)))
    key = (nseq, DEPTH)
    if key not in _NC_CACHE:
        _NC_CACHE[key] = build(nseq, DEPTH)
    nc = _NC_CACHE[key]
    in_maps = []
    for c in range(NCORES):
        d = dict(w)
        d["x"] = np.ascontiguousarray(x[c * nseq:(c + 1) * nseq])
        d["p"] = np.ascontiguousarray(p[:, c * nseq:(c + 1) * nseq])
        in_maps.append(d)
    res = run_bass_kernel_spmd(nc, in_maps, core_ids=list(range(NCORES)))
    return np.concatenate([r["out"] for r in res.results], axis=0)
```

```python
import contextlib
import math
import os
PH = int(os.environ.get('KPH', '99'))
KSKIP = os.environ.get('KSKIP', '')
import numpy as np
import concourse.bass as bass
import concourse.mybir as mybir
from concourse.bass_utils import run_bass_kernel_spmd

F32 = mybir.dt.float32
BF16 = mybir.dt.bfloat16
I32 = mybir.dt.int32
AF = mybir.ActivationFunctionType
ALU = mybir.AluOpType
ESZ = {F32: 4, BF16: 2, I32: 4}
ENGS = ("pe", "act", "dve", "pool", "sp")
PAGE = 128

NCORES = 8
DEPTH = 4
SEQ = 2048
D = 1024
DFF = 2816
NT = 4
TW = 512
ALPHA = (2.0 * DEPTH) ** 0.25
LN_EPS = 1e-5
WINS = (2, 4, 8, 16)
STAGES = [(1, 4), (4, 4), (16, 4), (64, 4), (256, 4), (1024, 2)]
PWLIST = []
for _s, _r in STAGES:
    for _k in range(1, _r):
        PWLIST.append(_s * _k)
PWIDX = {p: i for i, p in enumerate(PWLIST)}
FPASS = [(0, 8), (8, 8), (16, 6)]


class V:
    __slots__ = ("ap", "space", "lo", "hi")

    def __init__(self, ap, space, lo, hi):
        self.ap, self.space, self.lo, self.hi = ap, space, lo, hi


class Buf:
    def __init__(self, full_ap, space, base, cols, dtype):
        self.full, self.space, self.base, self.cols, self.dtype = full_ap, space, base, cols, dtype
        self.esz = ESZ[dtype]

    def v(self, c0=0, c1=None):
        c1 = self.cols if c1 is None else c1
        assert 0 <= c0 < c1 <= self.cols, (c0, c1, self.cols)
        return V(self.full[:, c0:c1], self.space, self.base + c0 * self.esz, self.base + c1 * self.esz)

    def v3(self, c0, n, stride, inner, i0=0):
        c0 += i0
        a0, r = divmod(c0, stride)
        assert self.cols % stride == 0 and r + inner <= stride and a0 + n <= self.cols // stride, (c0, n, stride, inner, self.cols)
        ap = self.full.rearrange("p (a b) -> p a b", b=stride)[:, a0:a0 + n, r:r + inner]
        c0 -= i0
        lo = self.base + (c0 + i0) * self.esz
        hi = self.base + (c0 + (n - 1) * stride + i0 + inner) * self.esz
        return V(ap, self.space, lo, hi)

    def parts(self, c0, n, stride, inner):
        return [self.v(c0 + a * stride, c0 + a * stride + inner) for a in range(n)]


class Rec:
    __slots__ = ("lo", "hi", "w", "sig", "dead")

    def __init__(self, lo, hi, w, sig):
        self.lo, self.hi, self.w, self.sig, self.dead = lo, hi, w, sig, False


class Op:
    __slots__ = ("fn", "waits", "sig", "inc")

    def __init__(self, fn, waits, sig, inc):
        self.fn, self.waits, self.sig, self.inc = fn, waits, sig, inc


class KB:
    def __init__(self, nc, sb_bytes, stack):
        self.nc = nc
        self.stack = stack
        self.ops = {e: [] for e in ENGS}
        self.pages = {"sb": {}, "ps": {}}
        self.waited = {e: {} for e in ENGS}
        self.sem_of = {}
        self.cnt = {}
        self.nsem = 0
        self.sb_words = sb_bytes // 4
        self.arena = stack.enter_context(nc.sbuf_tensor("arena", [128, self.sb_words], F32))
        self.psum = stack.enter_context(nc.psum_tensor("psum", [128, 8 * 512], F32))
        self.sb_top = 0
        self.nbank = 0
        self.defer = False
        self.deferred = []
        self.new_epoch()

    def alloc(self, cols, dtype):
        nbytes = (cols * ESZ[dtype] + 127) // 128 * 128
        base = self.sb_top
        self.sb_top += nbytes
        assert self.sb_top <= self.sb_words * 4, ("SBUF overflow", self.sb_top)
        return self.at(base, cols, dtype)

    def at(self, base, cols, dtype):
        assert base % 4 == 0
        w0 = base // 4
        nw = (cols * ESZ[dtype] + 3) // 4
        assert (w0 + nw) <= self.sb_words, ("SBUF overflow at", base, cols)
        ap = self.arena[:, w0:w0 + nw]
        if dtype != F32:
            ap = ap.bitcast(dtype)
        return Buf(ap, "sb", base, cols, dtype)

    def bank(self, i, dtype=F32):
        ap = self.psum[:, i * 512:(i + 1) * 512]
        cols = 512
        if dtype != F32:
            ap = ap.bitcast(dtype)
            cols = 512 * 4 // ESZ[dtype]
        return Buf(ap, "ps", i * 2048, cols, dtype)

    def nb(self, dtype=F32):
        b = self.bank(self.nbank % 8, dtype)
        self.nbank += 1
        return b

    def _newsem(self, name):
        s = self.stack.enter_context(self.nc.semaphore(f"{name}_{self.nsem}"))
        self.nsem += 1
        self.cnt[id(s)] = 0
        return s

    def new_epoch(self):
        for e in ("pe", "act", "dve", "pool"):
            self.sem_of[e] = self._newsem(e)

    def chan(self, name="dma"):
        return self._newsem(name)

    def _cands(self, v):
        pg = self.pages[v.space]
        seen = {}
        for p in range(v.lo // PAGE, (v.hi - 1) // PAGE + 1):
            lst = pg.get(p)
            if lst:
                for r in lst:
                    if not r.dead and r.lo < v.hi and v.lo < r.hi:
                        seen[id(r)] = r
        return seen.values()

    def _add(self, v, w, sig):
        r = Rec(v.lo, v.hi, w, sig)
        pg = self.pages[v.space]
        for p in range(v.lo // PAGE, (v.hi - 1) // PAGE + 1):
            lst = pg.get(p)
            if lst is None:
                pg[p] = [r]
            else:
                if len(lst) > 8:
                    lst[:] = [x for x in lst if not x.dead]
                lst.append(r)

    def flush(self, n=None):
        q = self.deferred
        k = len(q) if n is None else min(n, len(q))
        for _ in range(k):
            a = q.pop(0)
            self.emit(*a)

    def emit(self, eng, fn, reads=(), writes=(), chan=None):
        if self.defer:
            self.deferred.append((eng, fn, list(reads), list(writes), chan))
            return None
        if any(v.space == "ps" for v in reads) or any(v.space == "ps" for v in writes):
            nr, nw, seenb = [], [], set()
            for v in list(reads) + list(writes):
                if v.space == "ps":
                    for b in range(v.lo // 2048, (v.hi - 1) // 2048 + 1):
                        if b not in seenb:
                            seenb.add(b)
                            nw.append(V(None, "ps", b * 2048, b * 2048 + 2048))
            nr = [v for v in reads if v.space != "ps"]
            nw = nw + [v for v in writes if v.space != "ps"]
            reads, writes = nr, nw
        deps = {}
        for v in reads:
            for r in self._cands(v):
                if r.w:
                    k = id(r.sig[0])
                    if k not in deps or deps[k][1] < r.sig[1]:
                        deps[k] = r.sig
        for v in writes:
            for r in self._cands(v):
                k = id(r.sig[0])
                if k not in deps or deps[k][1] < r.sig[1]:
                    deps[k] = r.sig
        waits = []
        wd = self.waited[eng]
        own = self.sem_of.get(eng)
        for k, (sem, val) in deps.items():
            if eng == "pe" and sem is own:
                continue
            if wd.get(k, 0) >= val:
                continue
            wd[k] = val
            waits.append((sem, val))
        if chan is not None:
            prev = self.cnt[id(chan)]
            if prev > 0 and wd.get(id(chan), 0) < prev:
                wd[id(chan)] = prev
                waits.append((chan, prev))
            self.cnt[id(chan)] += 16
            sig = (chan, self.cnt[id(chan)])
            inc = 16
        else:
            sem = self.sem_of[eng]
            self.cnt[id(sem)] += 1
            sig = (sem, self.cnt[id(sem)])
            inc = 1
        self.ops[eng].append(Op(fn, waits, sig, inc))
        for v in writes:
            for r in self._cands(v):
                if v.lo <= r.lo and r.hi <= v.hi:
                    r.dead = True
            self._add(v, True, sig)
        for v in reads:
            for r in self._cands(v):
                if (not r.w) and r.sig[0] is sig[0] and v.lo <= r.lo and r.hi <= v.hi and chan is None:
                    r.dead = True
            self._add(v, False, sig)
        return sig

    def wait_sig(self, eng, sig):
        self.ops[eng].append(Op(None, [sig], None, 0))

    def finish(self):
        kb = self

        def run(engname):
            def body(eng):
                for op in kb.ops[engname]:
                    for (sem, val) in op.waits:
                        eng.wait_ge(sem, val)
                    if op.fn is None:
                        continue
                    ins = op.fn(eng)
                    ins.then_inc(op.sig[0], op.inc)
            return body

        with self.nc.Block() as block:
            block.tensor(run("pe"))
            block.scalar(run("act"))
            block.vector(run("dve"))
            block.gpsimd(run("pool"))
            block.sync(run("sp"))


def _aps(x):
    return x.ap if isinstance(x, V) else x


class Em:
    def __init__(self, kb):
        self.kb = kb

    def mm(self, out_v, terms):
        n = len(terms)

        def fn(e):
            ins = None
            for i, (l, r, o) in enumerate(terms):
                ins = e.matmul((o or out_v).ap, l.ap, r.ap, start=(i == 0), stop=(i == n - 1))
            return ins
        reads = [t[0] for t in terms] + [t[1] for t in terms]
        return self.kb.emit("pe", fn, reads=reads, writes=[out_v])

    def transpose(self, out_v, in_v, ident_v):
        return self.kb.emit("pe", lambda e: e.transpose(out=out_v.ap, in_=in_v.ap, identity=ident_v.ap),
                            reads=[in_v, ident_v], writes=[out_v])

    def act(self, out_v, in_v, func, scale=1.0, bias=None, rd=None, wr=None):
        reads = list(rd) if rd is not None else [in_v]
        kw = {}
        if isinstance(scale, V):
            reads.append(scale)
        if bias is not None:
            kw["bias"] = _aps(bias)
            if isinstance(bias, V):
                reads.append(bias)
        return self.kb.emit("act", lambda e: e.activation(out=out_v.ap, in_=in_v.ap, func=func, scale=_aps(scale), **kw),
                            reads=reads, writes=list(wr) if wr is not None else [out_v])

    def tt(self, eng, out_v, a, b, op, rd=None, wr=None):
        return self.kb.emit(eng, lambda e: e.tensor_tensor(out=out_v.ap, in0=a.ap, in1=b.ap, op=op),
                            reads=list(rd) if rd is not None else [a, b], writes=list(wr) if wr is not None else [out_v])

    def ts(self, eng, out_v, a, s1, op0, s2=None, op1=None, rd=None, wr=None):
        reads = list(rd) if rd is not None else [a]
        for s in (s1, s2):
            if isinstance(s, V):
                reads.append(s)
        kw = {}
        if op1 is not None:
            kw["op1"] = op1
        return self.kb.emit(eng, lambda e: e.tensor_scalar(out=out_v.ap, in0=a.ap, scalar1=_aps(s1), scalar2=_aps(s2), op0=op0, **kw),
                            reads=reads, writes=list(wr) if wr is not None else [out_v])

    def stt(self, out_v, a, s, b, op0, op1, rd=None, wr=None):
        reads = list(rd) if rd is not None else [a, b]
        if isinstance(s, V):
            reads.append(s)
        return self.kb.emit("dve", lambda e: e.scalar_tensor_tensor(out=out_v.ap, in0=a.ap, scalar=_aps(s), in1=b.ap, op0=op0, op1=op1),
                            reads=reads, writes=list(wr) if wr is not None else [out_v])

    def copy(self, eng, out_v, in_v, rd=None, wr=None):
        if eng == "act":
            f = lambda e: e.copy(out=out_v.ap, in_=in_v.ap)
        else:
            f = lambda e: e.tensor_copy(out=out_v.ap, in_=in_v.ap)
        return self.kb.emit(eng, f, reads=list(rd) if rd is not None else [in_v], writes=list(wr) if wr is not None else [out_v])

    def recip(self, out_v, in_v):
        return self.kb.emit("dve", lambda e: e.reciprocal(out=out_v.ap, in_=in_v.ap), reads=[in_v], writes=[out_v])

    def memset(self, eng, out_v, val):
        return self.kb.emit(eng, lambda e: e.memset(out_v.ap, val), writes=[out_v])

    def dma(self, eng, out, in_, chan, reads=(), writes=()):
        return self.kb.emit(eng, lambda e: e.dma_start(out=_aps(out), in_=_aps(in_)), reads=reads, writes=writes, chan=chan)


C_ID, C_SW, C_ONE = 0, 128, 256
C_SGN, C_NSGN = 384, 385
C_MASK = 386
C_RT = 394
C_E = 458
C_MLO, C_MHI = 522, 523
C_N = 524
VC_PB, VC_PS, VC_D, VC_GB, VC_L1G, VC_L1B, VC_L2G, VC_L2B, VC_N = 0, 4, 8, 12, 16, 24, 32, 40, 48


def make_consts():
    c = np.zeros((128, C_N), np.float32)
    c[:, C_ID:C_ID + 128] = np.eye(128, dtype=np.float32)
    for k in range(128):
        c[k, C_SW + (k + 64) % 128] = 1.0
    c[:, C_ONE:C_ONE + 128] = 1.0
    c[:64, C_SGN] = 1.0
    c[64:, C_SGN] = -1.0
    c[:, C_NSGN] = -c[:, C_SGN]
    for q in range(8):
        c[16 * q:16 * q + 16, C_MASK + q] = 1.0
    for wi, w in enumerate(WINS):
        for t in range(16):
            c[:, C_RT + wi * 16 + t] = np.float32(1.0) / np.float32(min(t + 1, w))
    for k in range(128):
        c[k, C_E + k % 64] = 1.0
    c[:64, C_MLO] = 1.0
    c[64:, C_MHI] = 1.0
    return c


def build(nseq, nlayer):
    nc = bass.Bass("TRN2", target_bir_lowering=False)

    def din(name, shape):
        return nc.dram_tensor(name, shape, F32, kind="ExternalInput").ap()

    x_d = din("x", [nseq, SEQ, D])
    p_d = din("p", [nlayer, nseq, SEQ, 256])
    consts_d = din("consts", [128, C_N])
    vecs_d = din("vecs", [nlayer, 128, VC_N])
    ssa_d = din("ssm_a", [nlayer, 128, 96])
    ssb_d = din("ssm_b", [nlayer, 128, 1024])
    ssc_d = din("ssm_c", [nlayer, 128, 512])
    win_d = din("w_in_t", [nlayer, 8, 128, 1024])
    wout_d = din("w_out_t", [nlayer, 8, 128, 1024])
    gate_d = din("gate_t", [nlayer, 8, 128, 1024])
    w1_d = din("w1_t", [nlayer, 22, 128, 1024])
    w3_d = din("w3_t", [nlayer, 22, 128, 1024])
    w2_d = din("w2_t", [nlayer, 3, 8, 128, 1024])
    ple_d = din("ple_t", [nlayer, 8, 128, 256])
    glu_d = din("glu_t", [nlayer, 128, 2048])
    poolw_d = din("pool_t", [nlayer, 128, 512])
    out_d = nc.dram_tensor("out", [nseq, SEQ, D], F32, kind="ExternalOutput").ap()

    with contextlib.ExitStack() as st:
        kb = KB(nc, 206 * 1024, st)
        em = Em(kb)
        cst = kb.alloc(C_N, F32)
        id16 = kb.alloc(128, BF16)
        one16 = kb.alloc(128, BF16)
        vecs = kb.alloc(VC_N, F32)
        hT32 = kb.alloc(8 * SEQ, F32)
        hT16 = kb.alloc(8 * SEQ, BF16)
        mix16 = kb.alloc(8 * SEQ, BF16)
        NS = 8
        ring = [kb.alloc(1024, BF16) for _ in range(NS)]
        glu16 = kb.alloc(2048, BF16)
        poolw16 = kb.alloc(512, BF16)
        ccpad = kb.alloc(8 * 128, BF16)
        S0 = kb.sb_top
        K = 1024

        def sc(off, cols, dtype):
            return kb.at(S0 + off, cols, dtype)
        u32 = sc(0, SEQ, F32)
        sA = sc(8 * K, SEQ, F32)
        sB = sc(16 * K, SEQ, F32)
        z16 = [sc(24 * K, TW, BF16), sc(25 * K, TW, BF16)]
        XD = [sc(5 * K * i, SEQ, BF16) for i in range(4)]
        HB = [sc(5 * K * i + 4 * K, TW, BF16) for i in range(4)]
        mats = [sc(20 * K + 3 * K * i, 12 * 128, BF16) for i in range(2)]
        bbtpad = sc(50 * K, 8 * 128, BF16)
        ussm32 = sc(26 * K, SEQ, F32)
        ussm16 = sc(34 * K, SEQ, BF16)
        bx = sc(26 * K, 1024, F32)
        cc32 = sc(30 * K, 512, F32)
        bb32 = sc(32 * K, 512, F32)
        tmp32 = sc(34 * K, 32 * 40, F32)
        pwr = sc(42 * K, 16 * 32, F32)
        pwi = sc(44 * K, 16 * 32, F32)
        bbt = sc(46 * K, 4 * 128, BF16)
        cc16 = sc(47 * K, 512, BF16)
        bb16 = sc(48 * K, 512, BF16)
        ssa = sc(49 * K, 96, F32)
        lx16 = sc(0, 8 * TW, BF16)
        lsq16 = sc(8 * K, 8 * TW, BF16)
        lmean = sc(16 * K, TW, F32)
        lt1 = sc(18 * K, TW, F32)
        lrstd = sc(20 * K, TW, F32)
        pstage = sc(0, 16 * 256, F32)
        pT16 = sc(16 * K, 2 * SEQ, BF16)
        fA = [sc(24 * K + 2 * K * i, TW, F32) for i in range(4)]
        xstage = [sc(0, D, F32), sc(4 * K, D, F32)]
        assert S0 + 50 * K <= 206 * K, S0

        ident32 = cst.v(C_ID, C_ID + 128)
        swap32 = cst.v(C_SW, C_SW + 128)

        ch_c = kb.chan("c")
        ch_ring = [kb.chan("r") for _ in range(NS)]
        ch_misc = [kb.chan("m") for _ in range(6)]
        ch_xin = [kb.chan("xi") for _ in range(2)]
        ch_out = [kb.chan("xo") for _ in range(2)]
        ch_p = kb.chan("p")

        em.dma("sp", cst.v(), consts_d, ch_c, writes=[cst.v()])
        em.copy("dve", id16.v(), ident32)
        em.copy("dve", one16.v(), cst.v(C_ONE, C_ONE + 128))
        if "m" not in KSKIP:
            em.memset("pool", ccpad.v(), 0.0)

        chunks = []
        for s in range(nseq):
            for l in range(nlayer):
                for m in range(8):
                    chunks.append((win_d[l, m], 8))
                for m in range(8):
                    chunks.append((wout_d[l, m], 8))
                for pi, (f0, fn_) in enumerate(FPASS):
                    for f in range(f0, f0 + fn_):
                        chunks.append((w1_d[l, f], 8))
                        chunks.append((w3_d[l, f], 8))
                    for m in range(8):
                        chunks.append((w2_d[l, pi, m], fn_))
                for m in range(8):
                    chunks.append((gate_d[l, m], 8))
                    chunks.append((ple_d[l, m], 2))
        wstate = {"issued": 0, "next": 0}

        def w_issue(upto):
            while wstate["issued"] < min(upto, len(chunks)):
                i = wstate["issued"]
                src, kt = chunks[i]
                slot = ring[i % NS]
                dst = slot.v(0, kt * 128)
                em.dma("pool", dst, src[:, 0:kt * 128], ch_ring[i % NS], writes=[dst])
                wstate["issued"] += 1

        def w_next(kt):
            i = wstate["next"]
            assert chunks[i][1] == kt, (i, chunks[i][1], kt)
            w_issue(i + NS - 1)
            wstate["next"] += 1
            return ring[i % NS]

        if PH >= 0:
            w_issue(NS - 2)

        def load_x(s):
            for tt in range(16):
                stg = xstage[tt % 2]
                em.dma("sp", stg.v(), x_d[s, tt * 128:(tt + 1) * 128, :], ch_xin[tt % 2], writes=[stg.v()])
                for half in range(2):
                    bk = kb.nb()
                    for j in range(4):
                        m = half * 4 + j
                        em.transpose(bk.v(j * 128, j * 128 + 128), stg.v(m * 128, m * 128 + 128), ident32)
                    o32 = hT32.v3((half * 4) * SEQ + tt * 128, 4, SEQ, 128)
                    o16 = hT16.v3((half * 4) * SEQ + tt * 128, 4, SEQ, 128)
                    src = bk.v3(0, 4, 128, 128)
                    wr32 = hT32.parts((half * 4) * SEQ + tt * 128, 4, SEQ, 128)
                    wr16 = hT16.parts((half * 4) * SEQ + tt * 128, 4, SEQ, 128)
                    em.copy("act", o32, src, rd=[bk.v()], wr=wr32)
                    em.copy("dve", o16, src, rd=[bk.v()], wr=wr16)

        def store_out(s):
            for tt in range(16):
                stg = xstage[tt % 2]
                for half in range(2):
                    bk = kb.nb()
                    for j in range(4):
                        m = half * 4 + j
                        em.transpose(bk.v(j * 128, j * 128 + 128), hT32.v(m * SEQ + tt * 128, m * SEQ + tt * 128 + 128), ident32)
                    eng = "act" if half == 0 else "dve"
                    em.copy(eng, stg.v(half * 512, half * 512 + 512), bk.v())
                sig = em.dma("sp", out_d[s, tt * 128:(tt + 1) * 128, :], stg.v(), ch_out[tt % 2], reads=[stg.v()])
                final_sigs[tt % 2] = sig

        final_sigs = [None, None]

        def ssm_prep(l, stg):
            bx = sc(stg, 1024, F32)
            cc32 = sc(stg + 4 * K, 512, F32)
            bb32 = sc(stg + 6 * K, 512, F32)
            tmp32 = sc(stg + 8 * K, 32 * 40, F32)
            em.dma("sp", ssa.v(), ssa_d[l], ch_misc[0], writes=[ssa.v()])
            em.dma("sp", bx.v(), ssb_d[l], ch_misc[1], writes=[bx.v()])
            em.dma("sp", cc32.v(), ssc_d[l], ch_misc[2], writes=[cc32.v()])
            T = [tmp32.v(32 * i, 32 * i + 32) for i in range(40)]
            are, aim, ldt = ssa.v(0, 32), ssa.v(32, 64), ssa.v(64, 96)
            arec, dt, tre, er, ang = T[0], T[1], T[2], T[3], T[4]
            em.ts("dve", arec, are, -1e-4, ALU.min)
            em.act(dt, ldt, AF.Exp)
            em.tt("dve", tre, arec, dt, ALU.mult)
            em.act(er, tre, AF.Exp)
            em.tt("dve", ang, aim, dt, ALU.mult)
            ki = kb.at(tmp32.base + 32 * 4 * 39, 32, I32)

            def sin_of(dst, src, shift, t0, t1):
                em.ts("dve", t0, src, shift, ALU.add)
                em.ts("dve", ki.v(), t0, 1.0 / (2 * math.pi), ALU.mult)
                em.copy("dve", t1, ki.v())
                em.stt(t1, t1, -2.0 * math.pi, t0, ALU.mult, ALU.add)
                em.ts("dve", t0, t1, math.pi, ALU.is_gt, -2.0 * math.pi, ALU.mult)
                em.tt("dve", t1, t1, t0, ALU.add)
                em.ts("dve", t0, t1, -math.pi, ALU.is_lt, 2.0 * math.pi, ALU.mult)
                em.tt("dve", t1, t1, t0, ALU.add)
                em.ts("dve", t1, t1, 3.1415925, ALU.min, -3.1415925, ALU.max)
                em.act(dst, t1, AF.Sin)
            sinv, cosv = T[5], T[6]
            sin_of(sinv, ang, 0.0, T[7], T[8])
            sin_of(cosv, ang, math.pi / 2, T[7], T[8])
            lr, li = T[9], T[10]
            em.tt("dve", lr, er, cosv, ALU.mult)
            em.tt("dve", li, er, sinv, ALU.mult)
            xm1, den, rden, a1, a2, cr, ci = T[11], T[12], T[13], T[14], T[15], T[16], T[17]
            em.ts("dve", xm1, lr, -1.0, ALU.add)
            em.tt("dve", den, arec, arec, ALU.mult)
            em.tt("dve", a1, aim, aim, ALU.mult)
            em.tt("dve", den, den, a1, ALU.add)
            em.recip(rden, den)
            em.tt("dve", a1, xm1, arec, ALU.mult)
            em.tt("dve", a2, li, aim, ALU.mult)
            em.tt("dve", a1, a1, a2, ALU.add)
            em.tt("dve", cr, a1, rden, ALU.mult)
            em.tt("dve", a1, li, arec, ALU.mult)
            em.tt("dve", a2, xm1, aim, ALU.mult)
            em.tt("dve", a1, a1, a2, ALU.subtract)
            em.tt("dve", ci, a1, rden, ALU.mult)
            c1, c2 = T[18], T[19]
            em.ts("dve", c1, cr, cst.v(C_SGN, C_SGN + 1), ALU.mult)
            em.ts("dve", c2, ci, -1.0, ALU.mult)
            c1b = V(c1.ap.unsqueeze(2).broadcast_to([128, 32, 16]), "sb", c1.lo, c1.hi)
            c2b = V(c2.ap.unsqueeze(2).broadcast_to([128, 32, 16]), "sb", c2.lo, c2.hi)
            bx1 = bx.v3(0, 32, 16, 16)
            bx2 = bx.v3(512, 32, 16, 16)
            em.tt("dve", bb32.v3(0, 32, 16, 16), bx1, c1b, ALU.mult)
            em.tt("dve", bx2, bx2, c2b, ALU.mult)
            em.tt("dve", bb16.v(), bb32.v(), bx.v(512, 1024), ALU.add)
            em.copy("dve", cc16.v(), cc32.v())
            bk = kb.nb(BF16)
            for t in range(4):
                em.transpose(bk.v(t * 128, t * 128 + 128), bb16.v(t * 128, t * 128 + 128), id16.v())
            em.copy("dve", bbt.v(), bk.v(0, 512))
            pw = {1: (lr, li)}
            nxt = [20]

            def newt():
                i = nxt[0]
                nxt[0] += 1
                return T[i]

            def csq(a):
                re, im = newt(), newt()
                em.tt("dve", T[38], a[1], a[1], ALU.mult)
                em.tt("dve", re, a[0], a[0], ALU.mult)
                em.tt("dve", re, re, T[38], ALU.subtract)
                em.stt(im, a[0], 2.0, a[1], ALU.mult, ALU.mult)
                return (re, im)

            def cmul(a, b, re, im):
                em.tt("dve", T[38], a[1], b[1], ALU.mult)
                em.tt("dve", re, a[0], b[0], ALU.mult)
                em.tt("dve", re, re, T[38], ALU.subtract)
                em.tt("dve", T[38], a[1], b[0], ALU.mult)
                em.tt("dve", im, a[0], b[1], ALU.mult)
                em.tt("dve", im, im, T[38], ALU.add)

            mlo, mhi = cst.v(C_MLO, C_MLO + 1), cst.v(C_MHI, C_MHI + 1)

            def store(p, a):
                i = PWIDX[p]
                c0, c1 = pwr.v(32 * i, 32 * i + 32), pwi.v(32 * i, 32 * i + 32)
                em.ts("dve", T[36], a[1], cst.v(C_NSGN, C_NSGN + 1), ALU.mult)
                em.ts("dve", T[37], T[36], mhi, ALU.mult)
                em.stt(c0, a[0], mlo, T[37], ALU.mult, ALU.add)
                em.ts("dve", T[37], a[0], mhi, ALU.mult)
                em.stt(c1, T[36], mlo, T[37], ALU.mult, ALU.add)
            cur = pw[1]
            store(1, cur)
            e = 1
            while e < 1024:
                nxt[0] = 20 + (int(math.log2(e)) % 2) * 6
                sq = csq(cur)
                store(2 * e, sq) if (2 * e) in PWIDX else None
                if (3 * e) in PWIDX:
                    t3 = (newt(), newt())
                    cmul(sq, cur, t3[0], t3[1])
                    store(3 * e, t3)
                cur = sq
                e *= 2

        def gen_mats(stage_i, g0, buf):
            s, r = STAGES[stage_i]
            e32 = cst.v(C_E, C_E + 64)
            eb = V(e32.ap.unsqueeze(1).broadcast_to([128, 4, 64]), "sb", e32.lo, e32.hi)
            for k in range(1, r):
                pi = PWIDX[s * k]
                for h, src in ((0, pwr), (1, pwi)):
                    a = src.v(32 * pi + g0, 32 * pi + g0 + 4)
                    ab = V(a.ap.unsqueeze(2).broadcast_to([128, 4, 64]), "sb", a.lo, a.hi)
                    c0 = (k - 1) * 128 + h * 64
                    o = buf.v3(c0, 4, 384, 64)
                    em.tt("pool", o, eb, ab, ALU.mult, wr=buf.parts(c0, 4, 384, 64))

        gm_t = [sc(38 * K, 512, F32), sc(40 * K, 512, F32)]

        def ssm_tile(l, mt, evq):
            for q in range(8):
                em.ts("pool", bbtpad.v(q * 128, q * 128 + 128), bbt.v(mt * 128, mt * 128 + 128), cst.v(C_MASK + q, C_MASK + q + 1), ALU.mult)
            for q in range(8):
                g = mt * 8 + q
                em.copy("pool", ccpad.v(q * 128 + q * 16, q * 128 + q * 16 + 16), cc16.v(g * 16, g * 16 + 16))
            dcol = vecs.v(VC_D + mt, VC_D + mt + 1)
            def evac(dst, bk):
                ev = evq[0] % 2
                evq[0] += 1
                em.copy("act" if ev == 0 else "dve", dst, bk.v())

            def evac_add(dst, bk, addsrc, c0=0):
                em.tt("dve", dst, bk.v(c0, TW), addsrc, ALU.add)

            for b in range(2):
                g0 = mt * 8 + b * 4
                gen_mats(0, g0, mats[0])

                def M(buf, gi, k):
                    return buf.v(gi * 384 + (k - 1) * 128, gi * 384 + k * 128)
                for gi in range(4):
                    q = b * 4 + gi
                    for r in range(4):
                        bk = kb.nb()
                        em.mm(bk.v(), [(bbtpad.v(q * 128, q * 128 + 128), ussm16.v(r * TW, r * TW + TW), None)])
                        em.copy("act", XD[gi].v(r * TW, r * TW + TW), bk.v())
                gen_mats(1, g0, mats[1])
                for gi in range(4):
                    bk = kb.nb()
                    fold = (gi % 2 == 0)
                    terms = [] if fold else [(id16.v(), XD[gi].v(3 * TW, 4 * TW), None)]
                    for k in range(1, 4):
                        terms.append((M(mats[0], gi, k), XD[gi].v((3 - k) * TW, (4 - k) * TW), None))
                    em.mm(bk.v(), terms)
                    if fold:
                        evac_add(HB[gi].v(), bk, XD[gi].v(3 * TW, 4 * TW))
                    else:
                        em.copy("act", HB[gi].v(), bk.v())
                for si in range(1, len(STAGES)):
                    s_, r_ = STAGES[si]
                    mb = mats[si % 2]
                    gen_mats(si + 1 if si + 1 < len(STAGES) else 0, g0, mats[(si + 1) % 2])
                    for gi in range(4):
                        bk = kb.nb()
                        sh1 = s_ // 4
                        fold = (gi % 2 == 0)
                        terms = [] if fold else [(id16.v(), HB[gi].v(), None)]
                        for k in range(1, r_):
                            sh = s_ * k // 4
                            terms.append((M(mb, gi, k), HB[gi].v(0, TW - sh), bk.v(sh, TW)))
                        if fold:
                            em.mm(bk.v(sh1, TW), terms)
                            evac_add(HB[gi].v(sh1, TW), bk, HB[gi].v(sh1, TW), sh1)
                        else:
                            em.mm(bk.v(), terms)
                            em.copy("act", HB[gi].v(), bk.v())
                m0 = mats[len(STAGES) % 2]
                for gi in range(4):
                    for r in (2, 1, 0):
                        bk = kb.nb()
                        fold = (gi % 2 == 0)
                        terms = [] if fold else [(id16.v(), XD[gi].v(r * TW, (r + 1) * TW), None)]
                        for k in range(1, r + 1):
                            terms.append((M(m0, gi, k), XD[gi].v((r - k) * TW, (r - k + 1) * TW), None))
                        terms.append((M(m0, gi, r + 1), HB[gi].v(0, TW - 1), bk.v(1, TW)))
                        if not fold:
                            em.mm(bk.v(), terms)
                            em.copy("act", XD[gi].v((r + 1) * TW, (r + 2) * TW), bk.v())
                        elif r == 0:
                            em.mm(bk.v(1, TW), terms)
                            evac_add(XD[gi].v(TW + 1, 2 * TW), bk, XD[gi].v(1, TW), 1)
                            em.copy("act", XD[gi].v(TW, TW + 1), XD[gi].v(0, 1))
                        else:
                            em.mm(bk.v(), terms)
                            evac_add(XD[gi].v((r + 1) * TW, (r + 2) * TW), bk, XD[gi].v(r * TW, (r + 1) * TW))
                for r in range(4):
                    bk = kb.nb()
                    terms = []
                    for gi in range(4):
                        src = HB[gi].v() if r == 3 else XD[gi].v((r + 1) * TW, (r + 2) * TW)
                        terms.append((ccpad.v((b * 4 + gi) * 128, (b * 4 + gi) * 128 + 128), src, None))
                    em.mm(bk.v(), terms)
                    u_ap = ussm32.full.rearrange("p (j r) -> p r j", r=4)[:, r, :]
                    uv = V(u_ap, "sb", ussm32.base, ussm32.base + 4 * SEQ)
                    if b == 0:
                        em.stt(uv, uv, dcol, bk.v(), ALU.mult, ALU.add)
                    else:
                        em.tt("dve", uv, uv, bk.v(), ALU.add)
            for n in range(NT):
                em.act(mix16.v((4 + mt) * SEQ + n * TW, (4 + mt) * SEQ + n * TW + TW), ussm32.v(n * TW, n * TW + TW), AF.Gelu_apprx_tanh)

        def pool_tile(l, g):
            w = WINS[g]
            src = u32
            bufs = [sA, sB]
            bi = 0
            step = 1
            while step < w:
                dst = bufs[bi]
                em.tt("dve", dst.v(step, SEQ), src.v(step, SEQ), src.v(0, SEQ - step), ALU.add)
                em.copy("pool", dst.v(0, step), src.v(0, step))
                src = dst
                bi ^= 1
                step *= 2
            zb = bufs[bi]
            em.stt(zb.v(), src.v(), 1.0 / w, u32.v(), ALU.mult, ALU.subtract)
            em.tt("dve", zb.v(0, 16), src.v(0, 16), cst.v(C_RT + g * 16, C_RT + g * 16 + 16), ALU.mult)
            em.tt("dve", zb.v(0, 16), zb.v(0, 16), u32.v(0, 16), ALU.subtract)
            for n in range(NT):
                zz = z16[n % 2]
                em.copy("act", zz.v(), zb.v(n * TW, n * TW + TW))
                bk = kb.nb()
                em.mm(bk.v(), [(poolw16.v(g * 128, g * 128 + 128), zz.v(), None)])
                em.ts("dve", mix16.v(g * SEQ + n * TW, g * SEQ + n * TW + TW), bk.v(),
                      vecs.v(VC_PB + g, VC_PB + g + 1), ALU.add, vecs.v(VC_PS + g, VC_PS + g + 1), ALU.mult)

        LX = [sc(0, 8 * TW, BF16), sc(16 * K, 8 * TW, BF16)]
        LSQ = [sc(8 * K, 8 * TW, BF16), sc(24 * K, 8 * TW, BF16)]
        LMEAN = [sc(32 * K, TW, F32), sc(34 * K, TW, F32)]
        LRSTD = [sc(36 * K, TW, F32), sc(38 * K, TW, F32)]
        LT1 = sc(40 * K, TW, F32)

        def layer_norm(gcol, bcol):
            def A(n):
                c0 = n * TW
                xin = hT32.v3(c0, 8, SEQ, TW)
                xparts = hT32.parts(c0, 8, SEQ, TW)
                lx16, lsq16, lmean, lrstd = LX[n % 2], LSQ[n % 2], LMEAN[n % 2], LRSTD[n % 2]
                em.act(lsq16.v3(0, 8, TW, TW), xin, AF.Square, rd=xparts, wr=[lsq16.v()])
                em.copy("pool", lx16.v3(0, 8, TW, TW), xin, rd=xparts, wr=[lx16.v()])
                bs = kb.nb()
                bq = kb.nb()
                em.mm(bs.v(), [(one16.v(), lx16.v(m * TW, m * TW + TW), None) for m in range(8)])
                em.mm(bq.v(), [(one16.v(), lsq16.v(m * TW, m * TW + TW), None) for m in range(8)])
                em.ts("dve", lmean.v(), bs.v(), 1.0 / D, ALU.mult)
                em.tt("dve", LT1.v(), lmean.v(), lmean.v(), ALU.mult)
                em.stt(LT1.v(), bq.v(), 1.0 / D, LT1.v(), ALU.mult, ALU.subtract)
                em.act(LT1.v(), LT1.v(), AF.Sqrt, bias=LN_EPS)
                em.recip(lrstd.v(), LT1.v())

            def B(n):
                c0 = n * TW
                lmean, lrstd = LMEAN[n % 2], LRSTD[n % 2]
                mb_ = V(lmean.v().ap.unsqueeze(1).broadcast_to([128, 4, TW]), "sb", lmean.base, lmean.base + 2048)
                rb_ = V(lrstd.v().ap.unsqueeze(1).broadcast_to([128, 4, TW]), "sb", lrstd.base, lrstd.base + 2048)
                for hf in range(2):
                    xin = hT32.v3(hf * 4 * SEQ + c0, 4, SEQ, TW)
                    xparts = hT32.parts(hf * 4 * SEQ + c0, 4, SEQ, TW)
                    em.tt("pool", xin, xin, mb_, ALU.subtract, rd=xparts + [lmean.v()], wr=xparts)
                    em.tt("dve", xin, xin, rb_, ALU.mult, rd=xparts + [lrstd.v()], wr=xparts)
                    for m in range(hf * 4, hf * 4 + 4):
                        hv = hT32.v(m * SEQ + c0, m * SEQ + c0 + TW)
                        em.act(hv, hv, AF.Identity, scale=vecs.v(gcol + m, gcol + m + 1), bias=vecs.v(bcol + m, bcol + m + 1))
                    em.copy("act", hT16.v3(hf * 4 * SEQ + c0, 4, SEQ, TW), xin, rd=xparts, wr=hT16.parts(hf * 4 * SEQ + c0, 4, SEQ, TW))
            A(0)
            A(1)
            B(0)
            A(2)
            B(1)
            A(3)
            B(2)
            B(3)

        def layer(s, l, first, next_prep):
            kb.new_epoch()
            em.dma("sp", vecs.v(), vecs_d[l], ch_misc[3], writes=[vecs.v()])
            em.dma("pool", glu16.v(), glu_d[l], ch_misc[4], writes=[glu16.v()])
            em.dma("pool", poolw16.v(), poolw_d[l], ch_misc[5], writes=[poolw16.v()])
            if PH < 1:
                return
            if first:
                ssm_prep(l, 26 * K)
            evq = [0]
            if PH < 2:
                return
            for m in range(8):
                wc = w_next(8)
                for n in range(NT):
                    bk = kb.nb()
                    em.mm(bk.v(), [(wc.v(k * 128, k * 128 + 128), hT16.v(k * SEQ + n * TW, k * SEQ + n * TW + TW), None) for k in range(8)])
                    if m < 4:
                        em.copy("act", u32.v(n * TW, n * TW + TW), bk.v())
                    else:
                        em.copy("act", ussm32.v(n * TW, n * TW + TW), bk.v())
                        o_ap = ussm16.full.rearrange("p (r j) -> p r j", r=4)[:, :, n * 128:(n + 1) * 128]
                        i_ap = bk.full.rearrange("p (j r) -> p r j", r=4)
                        em.copy("dve", V(o_ap, "sb", ussm16.base, ussm16.base + 4096), V(i_ap, "ps", bk.base, bk.base + 2048))
                if m < 4:
                    pool_tile(l, m)
                elif PH >= 3:
                    ssm_tile(l, m - 4, evq)
            if PH < 4:
                return
            for n in range(NT):
                bks = []
                for m in range(4):
                    bk = kb.nb()
                    em.mm(bk.v(), [(glu16.v(k * 512 + m * 128, k * 512 + m * 128 + 128),
                                    mix16.v((4 + k) * SEQ + n * TW, (4 + k) * SEQ + n * TW + TW), None) for k in range(4)])
                    bks.append(bk)
                for m in range(4):
                    sg = fA[m % 4]
                    em.act(sg.v(), bks[m].v(), AF.Sigmoid, bias=vecs.v(VC_GB + m, VC_GB + m + 1))
                    mv = mix16.v((4 + m) * SEQ + n * TW, (4 + m) * SEQ + n * TW + TW)
                    em.tt("dve", mv, mv, sg.v(), ALU.mult)
            for m in range(8):
                wc = w_next(8)
                for n in range(NT):
                    bk = kb.nb()
                    em.mm(bk.v(), [(wc.v(k * 128, k * 128 + 128), mix16.v(k * SEQ + n * TW, k * SEQ + n * TW + TW), None) for k in range(8)])
                    hv = hT32.v(m * SEQ + n * TW, m * SEQ + n * TW + TW)
                    em.stt(hv, hv, ALPHA, bk.v(), ALU.mult, ALU.add)
            if PH < 5:
                return
            layer_norm(VC_L1G, VC_L1B)
            if PH < 6:
                return
            em.dma("sp", pstage.v3(0, 16, 256, 256), p_d[l, s].rearrange("(t q) c -> q t c", q=128), ch_p, writes=[pstage.v()])
            for kt in range(2):
                for quarter in range(4):
                    bk = kb.nb()
                    for j in range(4):
                        tt = quarter * 4 + j
                        em.transpose(bk.v(j * 128, j * 128 + 128), pstage.v(tt * 256 + kt * 128, tt * 256 + kt * 128 + 128), ident32)
                    em.copy("act", pT16.v(kt * SEQ + quarter * 512, kt * SEQ + quarter * 512 + 512), bk.v())
            if PH < 7:
                return
            per_it = [0]
            for pi, (f0, fn_) in enumerate(FPASS):
                for fl in range(fn_):
                    w1c = w_next(8)
                    w3c = w_next(8)
                    for n in range(NT):
                        ba = kb.nb()
                        bb_ = kb.nb()
                        em.mm(ba.v(), [(w1c.v(k * 128, k * 128 + 128), hT16.v(k * SEQ + n * TW, k * SEQ + n * TW + TW), None) for k in range(8)])
                        em.mm(bb_.v(), [(w3c.v(k * 128, k * 128 + 128), hT16.v(k * SEQ + n * TW, k * SEQ + n * TW + TW), None) for k in range(8)])
                        sa_ = fA[(fl * NT + n) % 4]
                        em.act(sa_.v(), ba.v(), AF.Silu)
                        em.tt("dve", mix16.v(fl * SEQ + n * TW, fl * SEQ + n * TW + TW), sa_.v(), bb_.v(), ALU.mult)
                        kb.flush(per_it[0])
                for m in range(8):
                    w2c = w_next(fn_)
                    for n in range(NT):
                        bk = kb.nb()
                        em.mm(bk.v(), [(w2c.v(k * 128, k * 128 + 128), mix16.v(k * SEQ + n * TW, k * SEQ + n * TW + TW), None) for k in range(fn_)])
                        hv = hT32.v(m * SEQ + n * TW, m * SEQ + n * TW + TW)
                        if pi == 0:
                            em.stt(hv, hv, ALPHA, bk.v(), ALU.mult, ALU.add)
                        else:
                            em.tt("dve", hv, hv, bk.v(), ALU.add)
                        if pi == len(FPASS) - 1:
                            em.copy("act", hT16.v(m * SEQ + n * TW, m * SEQ + n * TW + TW), hv)
                        kb.flush(per_it[0])
                if pi == 0 and next_prep is not None:
                    kb.defer = True
                    ssm_prep(next_prep, 0)
                    kb.defer = False
                    per_it[0] = len(kb.deferred) // 100 + 1
            kb.flush()
            for m in range(8):
                wg = w_next(8)
                wp = w_next(2)
                for n in range(NT):
                    bg = kb.nb()
                    be = kb.nb()
                    em.mm(bg.v(), [(wg.v(k * 128, k * 128 + 128), hT16.v(k * SEQ + n * TW, k * SEQ + n * TW + TW), None) for k in range(8)])
                    em.mm(be.v(), [(wp.v(k * 128, k * 128 + 128), pT16.v(k * SEQ + n * TW, k * SEQ + n * TW + TW), None) for k in range(2)])
                    sg = fA[(m * NT + n) % 4]
                    em.act(sg.v(), bg.v(), AF.Sigmoid)
                    em.tt("dve", sg.v(), sg.v(), be.v(), ALU.mult)
                    hv = hT32.v(m * SEQ + n * TW, m * SEQ + n * TW + TW)
                    em.tt("dve", hv, hv, sg.v(), ALU.add)
            layer_norm(VC_L2G, VC_L2B)

        for s in range(nseq):
            if "x" in KSKIP:
                continue
            load_x(s)
            for l in range(nlayer):
                if PH >= 0:
                    idx = s * nlayer + l
                    layer(s, l, idx == 0, ((idx + 1) % nlayer) if idx + 1 < nseq * nlayer else None)
            store_out(s)
        for sig in final_sigs:
            if sig is not None:
                kb.wait_sig("sp", sig)
        kb.finish()
    return nc


def _tile_w(W):
    Kd, Md = W.shape
    return np.ascontiguousarray(W.reshape(Kd // 128, 128, Md // 128, 128).transpose(2, 1, 0, 3)).reshape(Md // 128, 128, Kd)


def prep_weights(inp, layers):
    f = lambda a: np.asarray(a, np.float32)
    L = len(layers)
    o = {}
    o["consts"] = make_consts()
    vec = np.zeros((L, 128, VC_N), np.float32)
    ssa = np.zeros((L, 128, 96), np.float32)
    ssb = np.zeros((L, 128, 1024), np.float32)
    ssc = np.zeros((L, 128, 512), np.float32)
    for i, l in enumerate(layers):
        def colize(v, ntile):
            return f(v).reshape(ntile, 128).T
        vec[i, :, VC_PB:VC_PB + 4] = colize(inp["pool_b"][l], 4)
        vec[i, :, VC_PS:VC_PS + 4] = colize(inp["pool_scale"][l], 4)
        vec[i, :, VC_D:VC_D + 4] = colize(inp["ssm_d"][l], 4)
        vec[i, :, VC_GB:VC_GB + 4] = colize(inp["ssm_glu_b"][l], 4)
        vec[i, :, VC_L1G:VC_L1G + 8] = colize(inp["ln1_g"][l], 8)
        vec[i, :, VC_L1B:VC_L1B + 8] = colize(inp["ln1_b"][l], 8)
        vec[i, :, VC_L2G:VC_L2G + 8] = colize(inp["ln2_g"][l], 8)
        vec[i, :, VC_L2B:VC_L2B + 8] = colize(inp["ln2_b"][l], 8)
        are = f(inp["ssm_a_re"][l]).T
        aim = f(inp["ssm_a_im"][l]).T
        ssa[i, :, 0:32] = np.concatenate([are, are], 0)
        ssa[i, :, 32:64] = np.concatenate([aim, aim], 0)
        ssa[i, :, 64:96] = np.broadcast_to(f(inp["ssm_log_dt"][l])[None, :], (128, 32))
        bre = f(inp["ssm_b_re"][l]).transpose(1, 0, 2).reshape(64, 512)
        bim = f(inp["ssm_b_im"][l]).transpose(1, 0, 2).reshape(64, 512)
        ssb[i, :, 0:512] = np.concatenate([bre, bim], 0)
        ssb[i, :, 512:1024] = np.concatenate([bim, bre], 0)
        cre = f(inp["ssm_c_re"][l]).transpose(2, 0, 1).reshape(64, 512)
        cim = f(inp["ssm_c_im"][l]).transpose(2, 0, 1).reshape(64, 512)
        ssc[i] = np.concatenate([cre, cim], 0)
    o["vecs"], o["ssm_a"], o["ssm_b"], o["ssm_c"] = vec, ssa, ssb, ssc
    o["w_in_t"] = np.stack([_tile_w(f(inp["w_in"][l])) for l in layers])
    o["w_out_t"] = np.stack([_tile_w(f(inp["w_out"][l])) for l in layers])
    o["gate_t"] = np.stack([_tile_w(f(inp["ple_gate_w"][l])) for l in layers])
    o["w1_t"] = np.stack([_tile_w(f(inp["ffn_w1"][l])) for l in layers])
    o["w3_t"] = np.stack([_tile_w(f(inp["ffn_w3"][l])) for l in layers])
    w2 = np.zeros((L, 3, 8, 128, 1024), np.float32)
    for i, l in enumerate(layers):
        W2 = f(inp["ffn_w2"][l])
        for pi, (f0, fn_) in enumerate(FPASS):
            w2[i, pi, :, :, :fn_ * 128] = _tile_w(W2[f0 * 128:(f0 + fn_) * 128, :])
    o["w2_t"] = w2
    o["ple_t"] = np.stack([_tile_w(f(inp["ple_w"][l])) for l in layers])
    o["glu_t"] = np.stack([np.ascontiguousarray(f(inp["ssm_glu_w"][l]).reshape(4, 128, 512).transpose(1, 0, 2)).reshape(128, 2048) for l in layers])
    o["pool_t"] = np.stack([np.ascontiguousarray(f(inp["pool_w"][l]).transpose(1, 0, 2)).reshape(128, 512) for l in layers])
    return o


_NC_CACHE = {}


def kernel(**inputs):
    x = np.asarray(inputs["x"], np.float32)
    p = np.asarray(inputs["p"], np.float32)
    B = x.shape[0]
    nseq = B // NCORES
    w = prep_weights(inputs, list(range(DEPTH)))
    key = (nseq, DEPTH)
    if key not in _NC_CACHE:
        _NC_CACHE[key] = build(nseq, DEPTH)
    nc = _NC_CACHE[key]
    in_maps = []
    for c in range(NCORES):
        d = dict(w)
        d["x"] = np.ascontiguousarray(x[c * nseq:(c + 1) * nseq])
        d["p"] = np.ascontiguousarray(p[:, c * nseq:(c + 1) * nseq])
        in_maps.append(d)
    res = run_bass_kernel_spmd(nc, in_maps, core_ids=list(range(NCORES)))
    return np.concatenate([r["out"] for r in res.results], axis=0)
```
